# Optimizing a Trainium2 kernel written in Bass

```python
import functools
import jax, jax.numpy as jnp
from jax import lax
import numpy as np

D_MODEL = 1024
BATCH = 8
SEQ = 2048
DEPTH = 1
DEC_BATCH = 128
DEC_SEQ = 1
PAST_LEN = 8192
PAGE_SIZE = 128

DIL_GROUPS = ((128, 1), (512, 4), (2048, 16))
N_GROUPS = 3
HEADS_PER_GROUP = 4
HEAD_DIM_A = 128
ATT_QKV = N_GROUPS * HEADS_PER_GROUP * HEAD_DIM_A
ATT_OUT = HEADS_PER_GROUP * HEAD_DIM_A
BAND_BLOCK = 128

RET_HEADS = 4
RET_DK = 128
RET_DV = 256
RET_QK = RET_HEADS * RET_DK
RET_V = RET_HEADS * RET_DV
RET_CHUNK = 128
ROPE_BASE = 10000.0

D_FF = 4 * D_MODEL
EPS = 1e-6
NEG_INF = -1e30

COL_SIZES = (ATT_QKV, ATT_QKV, ATT_QKV, RET_QK, RET_QK, RET_V, RET_V, D_MODEL, D_MODEL)
IN_COLS = 3 * ATT_QKV + 2 * RET_QK + 2 * RET_V + 2 * D_MODEL

kernel_name = "dilated_attn_retention_hybrid_step"


def _rmsnorm(x, g):
    xf = x.astype(jnp.float32)
    y = xf * lax.rsqrt(jnp.mean(xf * xf, axis=-1, keepdims=True) + EPS)
    return (y * g.astype(jnp.float32)).astype(x.dtype)


def _rope(x, pos):
    half = x.shape[-1] // 2
    inv = ROPE_BASE ** (-jnp.arange(half, dtype=jnp.float32) / half)
    ang = pos.astype(jnp.float32)[:, None] * inv[None, :]
    cos = jnp.cos(ang)[None, :, None, :]
    sin = jnp.sin(ang)[None, :, None, :]
    x1, x2 = x[..., :half], x[..., half:]
    return jnp.concatenate([x1 * cos - x2 * sin, x1 * sin + x2 * cos], axis=-1)


def _dilated_band(q, k, v, window, dil):
    B, S, H, Dh = q.shape
    n = S // dil
    span = window // dil
    nb = -(-n // BAND_BLOCK)
    n_pad = nb * BAND_BLOCK

    def to_blocks(t):
        t = t.astype(jnp.float32).reshape(B, n, dil, H, Dh).transpose(0, 2, 1, 3, 4)
        t = jnp.pad(t, ((0, 0), (0, 0), (0, n_pad - n), (0, 0), (0, 0)))
        return t.reshape(B, dil, nb, BAND_BLOCK, H, Dh)

    qb, kb, vb = to_blocks(q), to_blocks(k), to_blocks(v)

    def with_prev(t):
        prev = jnp.concatenate([jnp.zeros_like(t[:, :, :1]), t[:, :, :-1]], axis=2)
        return jnp.concatenate([prev, t], axis=3)

    kk, vv = with_prev(kb), with_prev(vb)
    blk = jnp.arange(nb)[:, None, None] * BAND_BLOCK
    qi = blk + jnp.arange(BAND_BLOCK)[None, :, None]
    ki = blk - BAND_BLOCK + jnp.arange(2 * BAND_BLOCK)[None, None, :]
    rel = qi - ki
    mask = (rel >= 0) & (rel <= span) & (ki >= 0)
    s = jnp.einsum('bcnqhd,bcnkhd->bcnhqk', qb, kk) * (Dh ** -0.5)
    s = jnp.where(mask[None, None, :, None], s, NEG_INF)
    lse = jax.nn.logsumexp(s, axis=-1)
    p = jnp.exp(s - lse[..., None])
    o = jnp.einsum('bcnhqk,bcnkhd->bcnqhd', p, vv)
    o = o.reshape(B, dil, n_pad, H, Dh)[:, :, :n].transpose(0, 2, 1, 3, 4).reshape(B, S, H, Dh)
    lse = lse.transpose(0, 1, 2, 4, 3).reshape(B, dil, n_pad, H)[:, :, :n]
    lse = lse.transpose(0, 2, 1, 3).reshape(B, S, H)
    return o, lse


def _dilated_gather(q, kv_new, kv_buf, window, dil):
    L = kv_buf.shape[1]
    T, Dh = q.shape[1], q.shape[-1]
    span = window // dil
    kv_all = jnp.concatenate([kv_buf.astype(jnp.float32), kv_new.astype(jnp.float32)], axis=1)
    pos_q = PAST_LEN + jnp.arange(T)
    key_pos = pos_q[:, None] - dil * jnp.arange(span + 1)[None, :]
    idx = key_pos - (PAST_LEN - L)
    valid = idx >= 0
    g = jnp.take(kv_all, jnp.maximum(idx, 0), axis=1)
    s = jnp.einsum('bthd,btjhd->bthj', q.astype(jnp.float32), g[:, :, :, 0]) * (Dh ** -0.5)
    s = jnp.where(valid[None, :, None, :], s, NEG_INF)
    lse = jax.nn.logsumexp(s, axis=-1)
    p = jnp.exp(s - lse[..., None])
    o = jnp.einsum('bthj,btjhd->bthd', p, g[:, :, :, 1])
    return o, lse


def _combine_groups(outs, lses):
    w = jax.nn.softmax(jnp.stack(lses, axis=0), axis=0)
    return jnp.sum(w[..., None] * jnp.stack(outs, axis=0), axis=0)


def _attend_prompt(qa, ka, va):
    T = qa.shape[1]
    outs, lses, rows = [], [], []
    for g, (win, dil) in enumerate(DIL_GROUPS):
        o, lse = _dilated_band(qa[:, :, g], ka[:, :, g], va[:, :, g], win, dil)
        outs.append(o)
        lses.append(lse)
        kv = jnp.stack([ka[:, :, g], va[:, :, g]], axis=2)
        rows.append(kv[:, T - min(win, T):])
    return _combine_groups(outs, lses), rows


def _attend_sample(qa, ka, va, caches):
    outs, lses, rows = [], [], []
    for g, (win, dil) in enumerate(DIL_GROUPS):
        kv_new = jnp.stack([ka[:, :, g], va[:, :, g]], axis=2)
        o, lse = _dilated_gather(qa[:, :, g], kv_new, caches[g], win, dil)
        outs.append(o)
        lses.append(lse)
        rows.append(kv_new)
    return _combine_groups(outs, lses), rows


def _ret_chunk(S, qkv, log_g):
    q, k, v = qkv
    C = q.shape[1]
    t = jnp.arange(C, dtype=jnp.float32)
    rel = t[:, None] - t[None, :]
    D = jnp.where(rel[None] >= 0, jnp.exp(log_g[:, None, None] * jnp.maximum(rel, 0.0)[None]), 0.0)
    s = jnp.einsum('bqhd,bkhd->bhqk', q, k) * D[None]
    o = jnp.einsum('bhqk,bkhv->bqhv', s, v)
    inner = jnp.exp(log_g[None, :] * (t[:, None] + 1.0))
    o = o + jnp.einsum('bqhd,bhdv->bqhv', q * inner[None, :, :, None], S)
    tail = jnp.exp(log_g[None, :] * (C - 1.0 - t[:, None]))
    S_new = jnp.exp(log_g * C)[None, :, None, None] * S + jnp.einsum('bkhd,bkhv->bhdv', k * tail[None, :, :, None], v)
    return S_new, o


def _retention(q, k, v, S0):
    B, T, H, dk = q.shape
    dv = v.shape[-1]
    C = RET_CHUNK if T % RET_CHUNK == 0 else T
    n = T // C
    log_g = jnp.log1p(-jnp.power(2.0, -5.0 - jnp.arange(H, dtype=jnp.float32)))

    def chunks(t):
        return t.reshape(B, n, C, H, t.shape[-1]).transpose(1, 0, 2, 3, 4)

    S_fin, o = lax.scan(functools.partial(_ret_chunk, log_g=log_g), S0.astype(jnp.float32),
                        (chunks(q), chunks(k), chunks(v)))
    o = o.transpose(1, 0, 2, 3, 4).reshape(B, T, H, dv)
    return o, S_fin


def _layer(x, pos, attend, S0, lw):
    (w_in, w_att_br, w_ret_br, w_out, w_up, w_down,
     g_ret_norm, g_pre_mix, g_post_mix, g_pre_mlp, g_post_mlp) = lw
    B, T, _ = x.shape
    h = _rmsnorm(x, g_pre_mix)
    z = h @ w_in
    splits, acc = [], 0
    for c in COL_SIZES[:-1]:
        acc += c
        splits.append(acc)
    qa, ka, va, qr, kr, vr, gr, gate_a, gate_r = jnp.split(z, splits, axis=-1)
    shp_a = (B, T, N_GROUPS, HEADS_PER_GROUP, HEAD_DIM_A)
    att, kv_rows = attend(qa.reshape(shp_a), ka.reshape(shp_a), va.reshape(shp_a))
    a = att.reshape(B, T, ATT_OUT).astype(x.dtype) @ w_att_br
    qr = _rope(qr.reshape(B, T, RET_HEADS, RET_DK).astype(jnp.float32), pos)
    kr = _rope(kr.reshape(B, T, RET_HEADS, RET_DK).astype(jnp.float32), pos) * (RET_DK ** -0.5)
    vr = vr.reshape(B, T, RET_HEADS, RET_DV).astype(jnp.float32)
    ro, S_new = _retention(qr, kr, vr, S0)
    mu = jnp.mean(ro, axis=-1, keepdims=True)
    var = jnp.mean(jnp.square(ro - mu), axis=-1, keepdims=True)
    rn = ((ro - mu) * lax.rsqrt(var + EPS)).reshape(B, T, RET_V) * g_ret_norm.astype(jnp.float32)
    r = (jax.nn.silu(gr) * rn.astype(x.dtype)) @ w_ret_br
    m = jax.nn.sigmoid(gate_a) * a + jax.nn.sigmoid(gate_r) * r
    x = x + _rmsnorm(m @ w_out, g_post_mix)
    u = jnp.square(jax.nn.relu(_rmsnorm(x, g_pre_mlp) @ w_up))
    x = x + _rmsnorm(u @ w_down, g_post_mlp)
    return x, kv_rows, S_new


def setup_inputs(seed: int = 0) -> dict:
    key = jax.random.key(seed)
    ks = jax.random.split(key, 20)
    f32 = jnp.float32
    nrm = lambda k, shp, sc: jax.random.normal(k, shp, f32) * sc
    L1, L2, L3 = (min(w, PAST_LEN) for w, _ in DIL_GROUPS)
    kv_shape = lambda L: (DEPTH, DEC_BATCH, L, 2, HEADS_PER_GROUP, HEAD_DIM_A)
    return {
        "x_prompt": nrm(ks[0], (BATCH, SEQ, D_MODEL), 1.0),
        "x_sample": nrm(ks[1], (DEC_BATCH, DEC_SEQ, D_MODEL), 1.0),
        "cache_kv_d1": nrm(ks[2], kv_shape(L1), 1.0),
        "cache_kv_d4": nrm(ks[3], kv_shape(L2), 1.0),
        "cache_kv_d16": nrm(ks[4], kv_shape(L3), 1.0),
        "state_ret": nrm(ks[5], (DEPTH, DEC_BATCH, RET_HEADS, RET_DK, RET_DV), 1.0),
        "w_in": nrm(ks[6], (DEPTH, D_MODEL, IN_COLS), D_MODEL ** -0.5),
        "w_att_br": nrm(ks[7], (DEPTH, ATT_OUT, D_MODEL), ATT_OUT ** -0.5),
        "w_ret_br": nrm(ks[8], (DEPTH, RET_V, D_MODEL), RET_V ** -0.5),
        "w_out": nrm(ks[9], (DEPTH, D_MODEL, D_MODEL), D_MODEL ** -0.5),
        "w_up": nrm(ks[10], (DEPTH, D_MODEL, D_FF), D_MODEL ** -0.5),
        "w_down": nrm(ks[11], (DEPTH, D_FF, D_MODEL), D_FF ** -0.5),
        "g_ret_norm": 1.0 + nrm(ks[12], (DEPTH, RET_V), 0.05),
        "g_pre_mix": 1.0 + nrm(ks[13], (DEPTH, D_MODEL), 0.05),
        "g_post_mix": 1.0 + nrm(ks[14], (DEPTH, D_MODEL), 0.05),
        "g_pre_mlp": 1.0 + nrm(ks[15], (DEPTH, D_MODEL), 0.05),
        "g_post_mlp": 1.0 + nrm(ks[16], (DEPTH, D_MODEL), 0.05),
    }


def reference(x_prompt, x_sample, cache_kv_d1, cache_kv_d4, cache_kv_d16, state_ret,
              w_in, w_att_br, w_ret_br, w_out, w_up, w_down,
              g_ret_norm, g_pre_mix, g_post_mix, g_pre_mlp, g_post_mlp):
    xp, xs = x_prompt, x_sample
    pos_p = jnp.arange(SEQ, dtype=jnp.int32)
    pos_s = PAST_LEN + jnp.arange(DEC_SEQ, dtype=jnp.int32)
    p_rows = [[], [], []]
    s_rows = [[], [], []]
    p_states, s_states = [], []
    for l in range(DEPTH):
        lw = (w_in[l], w_att_br[l], w_ret_br[l], w_out[l], w_up[l], w_down[l],
              g_ret_norm[l], g_pre_mix[l], g_post_mix[l], g_pre_mlp[l], g_post_mlp[l])
        S0 = jnp.zeros((BATCH, RET_HEADS, RET_DK, RET_DV), jnp.float32)
        xp, rows_p, Sp = _layer(xp, pos_p, _attend_prompt, S0, lw)
        attend_s = functools.partial(_attend_sample, caches=(cache_kv_d1[l], cache_kv_d4[l], cache_kv_d16[l]))
        xs, rows_s, Ss = _layer(xs, pos_s, attend_s, state_ret[l], lw)
        for g in range(N_GROUPS):
            p_rows[g].append(rows_p[g])
            s_rows[g].append(rows_s[g])
        p_states.append(Sp)
        s_states.append(Ss)
    kv_d1_p = jnp.stack(p_rows[0], axis=0)
    kv_d4_p = jnp.stack(p_rows[1], axis=0)
    kv_d16_p = jnp.stack(p_rows[2], axis=0)
    ret_p = jnp.stack(p_states, axis=0)
    kv_d1_s = jnp.stack(s_rows[0], axis=0)
    kv_d4_s = jnp.stack(s_rows[1], axis=0)
    kv_d16_s = jnp.stack(s_rows[2], axis=0)
    ret_s = jnp.stack(s_states, axis=0)
    return (xp, xs, kv_d1_p, kv_d4_p, kv_d16_p, ret_p, kv_d1_s, kv_d4_s, kv_d16_s, ret_s)
```

```python
import numpy as np
from contextlib import ExitStack
import concourse.bass as bass
import concourse.mybir as mybir
from concourse.bass_utils import run_bass_kernel_spmd

F32 = mybir.dt.float32
BF16 = mybir.dt.bfloat16
F32R = mybir.dt.float32r
AF = mybir.ActivationFunctionType
ALU = mybir.AluOpType
AX = mybir.AxisListType

D = 1024
SEQ = 2048
NSMP = 16
NTOK = SEQ + NSMP
PAST = 8192
OFF_QA, OFF_KA, OFF_VA = 0, 1536, 3072
OFF_QR, OFF_KR, OFF_VR, OFF_GR = 4608, 5120, 5632, 6656
OFF_GA, OFF_GM = 7680, 8704
IN_COLS = 9728
EPS = 1e-6
NEG = -30000.0
SC_A = 128 ** -0.5
GAM = [1.0 - 2.0 ** (-5 - h) for h in range(4)]
DILS = [1, 4, 16]
TB = [(0, 512), (512, 512), (1024, 512), (1536, 512), (2048, 16)]


class Buf:
    __slots__ = ("writers", "readers")

    def __init__(self):
        self.writers = set()
        self.readers = set()


class Prog:
    NDSEM = 24

    def __init__(self, nc):
        self.nc = nc
        self.order = ["pe", "act", "dve", "pool", "sp"]
        self.recs = {e: [] for e in self.order}
        self.ndma = 0
        self.bar = {}

    def _deps(self, eng, reads, writes, adds):
        deps = set()
        for b in reads:
            deps |= b.writers
        for b in writes:
            deps |= b.writers
            deps |= b.readers
        for b in adds:
            deps |= b.readers
        deps |= self.bar.pop(eng, set())
        return deps

    def _commit(self, tok, reads, writes, adds):
        for b in adds:
            if b.readers:
                b.writers = {tok}
                b.readers = set()
            else:
                b.writers.add(tok)
        for b in writes:
            b.writers = {tok}
            b.readers = set()
        for b in reads:
            b.readers.add(tok)

    def op(self, eng, fn, reads=(), writes=(), adds=()):
        deps = self._deps(eng, reads, writes, adds)
        idx = len(self.recs[eng])
        if eng == "pe":
            deps = {d for d in deps if not (d[0] == "e" and d[1] == "pe")}
        tok = ("e", eng, idx)
        self.recs[eng].append({"fn": fn, "deps": deps, "dma": None})
        self._commit(tok, reads, writes, adds)
        return tok

    def dma(self, eng, out, in_, reads=(), writes=(), adds=()):
        deps = self._deps(eng, reads, writes, adds)
        k = self.ndma
        self.ndma += 1
        if k >= self.NDSEM:
            deps.add(("d", k - self.NDSEM))
        tok = ("d", k)
        self.recs[eng].append({"fn": (lambda e: e.dma_start(out=out, in_=in_)), "deps": deps, "dma": k})
        self._commit(tok, reads, writes, adds)
        return tok

    def barrier(self):
        toks = set()
        for e in self.order:
            for i in range(len(self.recs[e]) - 1, -1, -1):
                if self.recs[e][i]["dma"] is None:
                    toks.add(("e", e, i))
                    break
        for k in range(max(0, self.ndma - self.NDSEM), self.ndma):
            toks.add(("d", k))
        for e in self.order:
            self.bar[e] = set(toks) | self.bar.get(e, set())

    def emit(self):
        nc = self.nc
        sig = {e: {} for e in self.order}
        for e in self.order:
            for r in self.recs[e]:
                for d in r["deps"]:
                    if d[0] == "e":
                        sig[d[1]][d[2]] = 0
        for e in self.order:
            c = 0
            for i in range(len(self.recs[e])):
                if i in sig[e]:
                    c += 1
                    sig[e][i] = c
        with ExitStack() as st:
            esem = {e: st.enter_context(nc.semaphore("es_" + e)) for e in self.order}
            dsem = [st.enter_context(nc.semaphore("ds%d" % i)) for i in range(self.NDSEM)]
            block = st.enter_context(nc.Block())
            ndma = self.ndma
            NS = self.NDSEM

            def body(e):
                def f(eng):
                    waited = {}
                    for i, r in enumerate(self.recs[e]):
                        need = {}
                        for d in r["deps"]:
                            if d[0] == "e":
                                key = ("e", d[1])
                                val = sig[d[1]][d[2]]
                            else:
                                key = ("d", d[1] % NS)
                                val = 16 * (d[1] // NS + 1)
                            if val > need.get(key, 0):
                                need[key] = val
                        for key, val in need.items():
                            if waited.get(key, 0) >= val:
                                continue
                            waited[key] = val
                            s = esem[key[1]] if key[0] == "e" else dsem[key[1]]
                            eng.wait_ge(s, val)
                        ins = r["fn"](eng)
                        if r["dma"] is not None:
                            ins.then_inc(dsem[r["dma"] % NS], 16)
                        elif i in sig[e]:
                            ins.then_inc(esem[e], 1)
                    if e == "sp":
                        for j in range(min(NS, ndma)):
                            cnt = (ndma - 1 - j) // NS + 1
                            if waited.get(("d", j), 0) < 16 * cnt:
                                eng.wait_ge(dsem[j], 16 * cnt)
                return f

            block.tensor(body("pe"))
            block.scalar(body("act"))
            block.vector(body("dve"))
            block.gpsimd(body("pool"))
            block.sync(body("sp"))


class Ring:
    def __init__(self, aps):
        self.items = [(a, Buf()) for a in aps]
        self.i = 0

    def next(self):
        it = self.items[self.i % len(self.items)]
        self.i += 1
        return it


class Arena:
    def __init__(self, t, lo, hi):
        self.t, self.lo, self.hi, self.cur = t, lo, hi, lo

    def sub(self, lo, hi):
        return Arena(self.t, lo, hi)

    def get(self, shape, dt):
        esz = 2 if dt == BF16 else 4
        n = 1
        for s in shape[1:]:
            n *= s
        nb = (n * esz + 31) // 32 * 32
        off = self.cur
        assert off + nb <= self.hi, ("arena overflow", shape, off, nb, self.hi)
        self.cur = off + nb
        ap = self.t[:, off // 4: off // 4 + (n * esz + 3) // 4]
        if dt != F32:
            ap = ap.bitcast(dt)
        if len(shape) == 3:
            ap = ap.rearrange("p (a b) -> p a b", b=shape[2])
        elif len(shape) == 4:
            ap = ap.rearrange("p (a b c) -> p a b c", b=shape[2], c=shape[3])
        if shape[0] < 128:
            ap = ap[0:shape[0]]
        return ap


def KB(x):
    return int(x * 1024)


def build_program(stop_after=None):
    nc = bass.Bass("TRN2", target_bir_lowering=False)

    def din(name, shape):
        return nc.dram_tensor(name, shape, F32, kind="ExternalInput").ap()

    def dout(name, shape):
        return nc.dram_tensor(name, shape, F32, kind="ExternalOutput").ap()

    xp = din("xp", [SEQ, D])
    xs = din("xs", [NSMP, D])
    caches = [din("c1", [NSMP, 128, 1024]), din("c4", [NSMP, 512, 1024]), din("c16", [NSMP, 2048, 1024])]
    st_in = din("st", [NSMP, 4, 128, 256])
    w_in = din("w_in", [D, IN_COLS])
    w_att = din("w_att", [512, D])
    w_ret = din("w_ret", [D, D])
    w_out = din("w_out", [D, D])
    w_up = din("w_up", [D, 4 * D])
    w_down = din("w_down", [4 * D, D])
    k_ident = din("k_ident", [128, 128])
    k_maskb = din("k_maskb", [128, 256])
    k_cos = din("k_cos", [128, 17 * 64])
    k_sin = din("k_sin", [128, 17 * 64])
    k_dt = din("k_dt", [128, 512])
    k_small = din("k_small", [128, 8])
    k_gpre = din("k_gpre", [128, 16])
    k_grep = din("k_grep", [128, 3 * D])
    k_sel = din("k_sel", [16, 16 * 128])
    k_i16s = din("k_i16s", [16, 16])
    k_i16bc = din("k_i16bc", [128, 256])

    yp = dout("yp", [SEQ, D])
    ys = dout("ys", [NSMP, D])
    kvp = [dout("kv1p", [128, 1024]), dout("kv4p", [512, 1024]), dout("kv16p", [2048, 1024])]
    stp = dout("stp", [4, 128, 256])
    kvs = [dout("kv1s", [NSMP, 1024]), dout("kv4s", [NSMP, 1024]), dout("kv16s", [NSMP, 1024])]
    sts = dout("sts", [NSMP, 4, 128, 256])
    x1d = nc.dram_tensor("x1_scratch", [NTOK, D], F32).ap()

    P = Prog(nc)
    TOTAL = 212736
    with ExitStack() as st:
        ar_t = st.enter_context(nc.sbuf_tensor("arena", [128, TOTAL // 4], F32))
        ps_t = [st.enter_context(nc.psum_tensor("ps%d" % i, [128, 1024], F32)) for i in range(4)]
        bankB = [Buf() for _ in range(8)]

        def bank_ap(i):
            return ps_t[i // 2][:, (i % 2) * 512:(i % 2) * 512 + 512]

        class PsumPools:
            def set(self, singles, doubles):
                self.s = Ring([bank_ap(i) for i in singles])
                self.s.items = [(bank_ap(i), bankB[i]) for i in singles]
                self.d_items = [(ps_t[j][:, :], [bankB[2 * j], bankB[2 * j + 1]]) for j in doubles]
                self.di = 0

            def one(self):
                return self.s.next()

            def two(self):
                it = self.d_items[self.di % len(self.d_items)]
                self.di += 1
                return it

        PS = PsumPools()

        A = Arena(ar_t, 0, TOTAL)
        cA = A.sub(0, KB(3))
        ident_f = cA.get([128, 128], F32)
        ident_b = cA.get([128, 128], BF16)
        ones_b = cA.get([128, 128], BF16)
        ones_f = cA.get([128, 128], F32)
        gpre = cA.get([128, 16], F32)
        junk_lo = KB(3)
        X0 = KB(3)
        X1 = X0 + KB(32.5)
        X2 = X1 + KB(16.5)
        X3 = X2 + KB(32.5)
        X4 = X3 + KB(32.5)
        hT = A.sub(X0, X1).get([128, 8, NTOK], BF16)
        attT = A.sub(X1, X2).get([128, 4, NTOK], BF16)
        rgT = A.sub(X2, X3).get([128, 8, NTOK], BF16)
        mT = A.sub(X3, X4).get([128, 8, NTOK], BF16)
        h2T = hT

        def mm(out, lhsT, rhs, start, stop, reads=(), adds=()):
            P.op("pe", lambda e: e.matmul(out, lhsT=lhsT, rhs=rhs, start=start, stop=stop), reads=reads, adds=adds)

        def tr(out, in_, ident, reads=(), adds=()):
            P.op("pe", lambda e: e.transpose(out=out, in_=in_, identity=ident), reads=reads, adds=adds)

        def act(out, in_, func, reads=(), writes=(), adds=(), **kw):
            P.op("act", lambda e: e.activation(out=out, in_=in_, func=func, **kw), reads=reads, writes=writes, adds=adds)

        def tt(eng, out, in0, in1, op, reads=(), writes=(), adds=()):
            P.op(eng, lambda e: e.tensor_tensor(out=out, in0=in0, in1=in1, op=op), reads=reads, writes=writes, adds=adds)

        def ts(eng, out, in0, s1, s2, op0, op1=None, reads=(), writes=(), adds=()):
            if op1 is None:
                P.op(eng, lambda e: e.tensor_scalar(out=out, in0=in0, scalar1=s1, scalar2=None, op0=op0),
                     reads=reads, writes=writes, adds=adds)
            else:
                P.op(eng, lambda e: e.tensor_scalar(out=out, in0=in0, scalar1=s1, scalar2=s2, op0=op0, op1=op1),
                     reads=reads, writes=writes, adds=adds)

        def stt(out, in0, scalar, in1, op0, op1, reads=(), writes=(), adds=()):
            P.op("dve", lambda e: e.scalar_tensor_tensor(out=out, in0=in0, scalar=scalar, in1=in1, op0=op0, op1=op1),
                 reads=reads, writes=writes, adds=adds)

        def cp(eng, out, in_, reads=(), writes=(), adds=()):
            if eng == "act":
                act(out, in_, AF.Copy, reads=reads, writes=writes, adds=adds)
            else:
                P.op(eng, lambda e: e.tensor_copy(out=out, in_=in_), reads=reads, writes=writes, adds=adds)

        def recip(out, in_, reads=(), writes=()):
            P.op("dve", lambda e: e.reciprocal(out=out, in_=in_), reads=reads, writes=writes)

        def wload(dst, src2d, writes):
            P.dma("pool", dst, src2d.rearrange("(kc p) n -> p kc n", p=128), writes=writes)

        def rms_rstd(src, np_, statring, junk, reads):
            stt_, Bs = statring.next()
            act(junk[:np_], src, AF.Square, reads=reads, writes=[Bs], accum_out=stt_[:np_, 0:1])
            act(stt_[:np_, 1:2], stt_[:np_, 0:1], AF.Sqrt, reads=[Bs], writes=[Bs], scale=1.0 / D, bias=EPS)
            recip(stt_[:np_, 2:3], stt_[:np_, 1:2], reads=[Bs], writes=[Bs])
            return stt_[:np_, 2:3], Bs

        def norm_p1(src, np_, R, reads):
            rstd, Bs = rms_rstd(src, np_, R["stat"], R["junk"], reads)
            hb, Bhb = R["hb"].next()
            ts("dve", hb[:np_], src, rstd, None, ALU.mult, reads=list(reads) + [Bs], writes=[Bhb])
            return hb, Bhb

        def norm_p2(hb, Bhb, np_, tok0, gcol0, dstT, wB=None):
            pT, BpT = PS.one()
            pTv = pT.bitcast(BF16).rearrange("p (a b) -> p a b", b=128)
            for kc in range(8):
                tr(pTv[:, kc, :np_], hb[:np_, kc * 128:(kc + 1) * 128], ident_b[:np_, :np_], reads=[Bhb], adds=[BpT])
            tt("dve", dstT[:, :, tok0:tok0 + np_], pTv[:, :, :np_],
               gpre[:, gcol0:gcol0 + 8].unsqueeze(2).to_broadcast([128, 8, np_]), ALU.mult, reads=[BpT],
               writes=([wB] if wB is not None else ()))

        PS.set(list(range(8)), [0, 1, 2, 3])
        Bc = Buf()
        P.dma("sp", ident_f, k_ident, adds=[Bc])
        P.dma("sp", gpre, k_gpre, adds=[Bc])
        cp("dve", ident_b, ident_f, reads=[Bc], writes=[Bc])
        P.op("dve", lambda e: e.memset(ones_b, 1.0), adds=[Bc])
        P.op("dve", lambda e: e.memset(ones_f, 1.0), adds=[Bc])
        P.barrier()

        BhT = [Buf() for _ in range(17)]

        def phaseA(gen):
            T = A.sub(TOTAL - KB(16.5), TOTAL)
            R = {"stat": Ring([T.get([128, 4], F32) for _ in range(4)]),
                 "junk": T.get([128, 1024], F32),
                 "hb": Ring([T.get([128, 1024], BF16) for _ in range(2)])}
            next(gen)
            next(gen)
            xr = Ring([T.get([128, 1024], F32) for _ in range(2)])
            pend = None
            for t in range(17):
                np_ = 128 if t < 16 else 16
                src = xp[t * 128:(t + 1) * 128, :] if t < 16 else xs
                xt, Bx = xr.next()
                P.dma("sp", xt[:np_], src, writes=[Bx])
                hb, Bhb = norm_p1(xt[:np_], np_, R, [Bx])
                if pend is not None:
                    norm_p2(*pend)
                    if t - 1 < 15:
                        next(gen)
                pend = (hb, Bhb, np_, t * 128, 0, hT, BhT[t])
            norm_p2(*pend)
            return next(gen)

        qTs = kTs = vTs = None

        def phaseB1():
            T = A.sub(X2, TOTAL)
            vtm = [T.get([128, 16, 512], BF16) for _ in range(3)]
            Bv = [[Buf() for _ in range(16)] for _ in range(3)]
            wr = Ring([T.get([128, 8, 512], BF16) for _ in range(3)])
            kvr = Ring([T.get([128, 2, 512], F32) for _ in range(2)])
            kvsr = Ring([T.get([16, 2, 512], F32) for _ in range(2)])
            acc = T.get([128, 2, SEQ], F32)
            Bacc = [Buf() for _ in range(4)]
            qr_ = Ring([T.get([128, SEQ], BF16) for _ in range(2)])
            kr_ = Ring([T.get([128, SEQ], BF16) for _ in range(2)])
            cr = Ring([T.get([128, 8, 128], BF16) for _ in range(6)])
            ptr = Ring([T.get([128, 256], BF16) for _ in range(4)])
            maskf = T.get([128, 256], F32)
            maskb = T.get([128, 256], BF16)
            sm = T.get([128, 3, 12, 16], F32)
            Bsm = Buf()
            Bm = Buf()
            pre_w = []
            for off_ in (OFF_KA, OFF_VA):
                W_, BW_ = wr.next()
                wload(W_, w_in[:, off_:off_ + 512], [BW_])
                pre_w.append((W_, BW_))
            yield None
            P.dma("sp", maskf, k_maskb, writes=[Bm])
            cp("dve", maskb, maskf, reads=[Bm], writes=[Bm])

            for g in range(3):
                d = DILS[g]
                n = SEQ // d
                nb = n // 128
                if g == 0:
                    (Wk, BWk), (Wv, BWv) = pre_w
                else:
                    Wk, BWk = wr.next()
                    wload(Wk, w_in[:, OFF_KA + g * 512: OFF_KA + (g + 1) * 512], [BWk])
                    Wv, BWv = wr.next()
                    wload(Wv, w_in[:, OFF_VA + g * 512: OFF_VA + (g + 1) * 512], [BWv])
                for c in range(d):
                    for b in range(nb):
                        if g == 0:
                            yield None
                        hdep = [BhT[b]] if g == 0 else BhT[0:16]
                        tile = c * nb + b
                        start = b * 128 * d + c
                        tok = slice(start, start + 128 * d, d)
                        need_k = True if g == 2 else (b == nb - 1)
                        pv, Bpv = PS.one()
                        for kc in range(8):
                            mm(pv, hT[:, kc, tok], Wv[:, kc, :], kc == 0, kc == 7, reads=[BWv] + hdep, adds=[Bpv])
                        cp("act", vtm[g][:, tile, :], pv, reads=[Bpv], writes=[Bv[g][tile]])
                        if need_k:
                            pk, Bpk = PS.one()
                            for kc in range(8):
                                mm(pk, hT[:, kc, tok], Wk[:, kc, :], kc == 0, kc == 7, reads=[BWk] + hdep, adds=[Bpk])
                            kv, Bkv = kvr.next()
                            cp("dve", kv[:, 0, :], pk, reads=[Bpk], writes=[Bkv])
                            cp("dve", kv[:, 1, :], pv, reads=[Bpv, Bv[g][tile]], adds=[Bkv])
                            if g == 0:
                                dst = kvp[0]
                            else:
                                dst = kvp[g].rearrange("(i c) n -> c i n", c=d)[c]
                            P.dma("sp", dst, kv.rearrange("p a n -> p (a n)"), reads=[Bkv])
                pk, Bpk = PS.one()
                pv, Bpv = PS.one()
                for kc in range(8):
                    mm(pk[:16, :], hT[:, kc, SEQ:NTOK], Wk[:, kc, :], kc == 0, kc == 7, reads=[BWk, BhT[16]], adds=[Bpk])
                for kc in range(8):
                    mm(pv[:16, :], hT[:, kc, SEQ:NTOK], Wv[:, kc, :], kc == 0, kc == 7, reads=[BWv, BhT[16]], adds=[Bpv])
                kv, Bkv = kvsr.next()
                cp("dve", kv[:, 0, :], pk[:16, :], reads=[Bpk], writes=[Bkv])
                cp("dve", kv[:, 1, :], pv[:16, :], reads=[Bpv], adds=[Bkv])
                P.dma("sp", kvs[g], kv.rearrange("p a n -> p (a n)"), reads=[Bkv])

            for h in range(4):
                for g in range(3):
                    d = DILS[g]
                    n = SEQ // d
                    nb = n // 128
                    col = (g * 4 + h) * 128
                    Wq, BWq = cr.next()
                    wload(Wq, w_in[:, OFF_QA + col: OFF_QA + col + 128], [BWq])
                    Wk, BWk = cr.next()
                    wload(Wk, w_in[:, OFF_KA + col: OFF_KA + col + 128], [BWk])
                    Wv, BWv = cr.next()
                    wload(Wv, w_in[:, OFF_VA + col: OFF_VA + col + 128], [BWv])
                    qT, BqT = qr_.next()
                    kT, BkT = kr_.next()
                    for (W, BW, dst, Bdst, eng) in ((Wq, BWq, qT, BqT, "act"), (Wk, BWk, kT, BkT, "dve")):
                        for blk in range(4):
                            pq, Bpq = PS.one()
                            for kc in range(8):
                                mm(pq, W[:, kc, :], hT[:, kc, blk * 512:(blk + 1) * 512], kc == 0, kc == 7,
                                   reads=[BW], adds=[Bpq])
                            if d == 1:
                                o_ap, i_ap = dst[:, blk * 512:(blk + 1) * 512], pq
                            else:
                                w_ = 512 // d
                                o_ap = dst.rearrange("p (c i) -> p i c", c=d)[:, blk * w_:(blk + 1) * w_, :]
                                i_ap = pq.rearrange("p (i c) -> p i c", c=d)
                            cp(("act" if (eng == "dve" and blk % 2 == 1) else eng), o_ap, i_ap, reads=[Bpq], adds=[Bdst])
                    for j, (W, BW) in enumerate(((Wq, BWq), (Wk, BWk), (Wv, BWv))):
                        pq, Bpq = PS.one()
                        for kc in range(8):
                            mm(pq[:, 0:16], W[:, kc, :], hT[:, kc, SEQ:NTOK], kc == 0, kc == 7, reads=[BW], adds=[Bpq])
                        cp("act", sm[:, j, g * 4 + h, :], pq[:, 0:16], reads=[Bpq], adds=[Bsm])
                    tiles_ = [(c, b) for c in range(d) for b in range(nb)]

                    def att_s1(c, b, kT=kT, qT=qT, BkT=BkT, BqT=BqT, n=n):
                        pos0 = c * n + b * 128
                        lo = 0 if b > 0 else 128
                        sp_, Bsp = PS.one()
                        mm(sp_[:, 128:256], kT[:, pos0:pos0 + 128], qT[:, pos0:pos0 + 128], True, False,
                           reads=[BkT, BqT], adds=[Bsp])
                        if b > 0:
                            mm(sp_[:, 0:128], kT[:, pos0 - 128:pos0], qT[:, pos0:pos0 + 128], False, False,
                               reads=[BkT, BqT], adds=[Bsp])
                        mm(sp_[:, lo:256], ident_b, maskb[:, lo:256], False, True, reads=[Bm], adds=[Bsp])
                        pt, Bpt = ptr.next()
                        act(pt[:, lo:256], sp_[:, lo:256], AF.Exp, reads=[Bsp], writes=[Bpt], scale=SC_A)
                        return pt, Bpt

                    def att_s2(c, b, pt, Bpt, g=g, h=h, nb=nb):
                        tile = c * nb + b
                        ud, Bud = PS.one()
                        udv = ud[:, 0:256].rearrange("p (a q) -> p a q", a=2)
                        hs = slice(h * 128, (h + 1) * 128)
                        mm(udv[:, 0, :], vtm[g][:, tile, hs], pt[:, 128:256], True, False,
                           reads=[Bv[g][tile], Bpt], adds=[Bud])
                        if b > 0:
                            mm(udv[:, 0, :], vtm[g][:, tile - 1, hs], pt[:, 0:128], False, False,
                               reads=[Bv[g][tile - 1], Bpt], adds=[Bud])
                        mm(udv[:, 1, :], ones_b, pt[:, 128:256], False, b == 0, reads=[Bpt], adds=[Bud])
                        if b > 0:
                            mm(udv[:, 1, :], ones_b, pt[:, 0:128], False, True, reads=[Bpt], adds=[Bud])
                        if g == 0:
                            cp("dve", acc[:, :, b * 128:(b + 1) * 128], udv, reads=[Bud], adds=[Bacc[b // 4]])
                        elif g == 1:
                            av = acc[:, :, b * 512 + c:(b + 1) * 512:4]
                            tt("dve", av, av, udv, ALU.add, reads=[Bud, Bacc[b]], writes=[Bacc[b]])
                        else:
                            av = acc[:, :, c:SEQ:16]
                            tt("dve", av, av, udv, ALU.add, reads=[Bud] + Bacc, writes=Bacc)

                    LOOK = 2
                    pend_ = []
                    for i_, (c, b) in enumerate(tiles_):
                        pend_.append((c, b) + att_s1(c, b))
                        if len(pend_) > LOOK:
                            att_s2(*pend_.pop(0))
                    while pend_:
                        att_s2(*pend_.pop(0))
                recip(acc[:, 1, :], acc[:, 1, :], reads=Bacc, writes=Bacc)
                tt("dve", attT[:, h, 0:SEQ], acc[:, 0, :], acc[:, 1, :], ALU.mult, reads=Bacc)
            P.barrier()
            yield sm

        def phaseB1c(sm):
            PS.set([2, 3, 4, 5, 6, 7], [0])
            T = A.sub(X2, X2 + KB(120))
            qtm = [T.get([16, 512], BF16) for _ in range(3)]
            Bq = [Buf() for _ in range(3)]
            self_f = T.get([16, 16, 128], F32)
            sel = T.get([16, 16, 128], BF16)
            Bsel = Buf()
            P.dma("sp", self_f, k_sel.rearrange("k (s m) -> k s m", m=128), writes=[Bsel])
            cp("dve", sel, self_f, reads=[Bsel], writes=[Bsel])
            kvg_t = [T.get([128, 16, 1024], BF16) for _ in range(2)]
            kvg_B = [[Buf() for _ in range(NSMP)] for _ in range(2)]
            prr = Ring([T.get([128, 512], F32) for _ in range(3)])
            Sc = T.get([128, 3, 16, 4], F32)
            E = T.get([128, 3, 16, 4], BF16)
            BSc = [Buf() for _ in range(3)]
            BE = [Buf() for _ in range(3)]
            e0 = T.get([128, 12, 16], F32)
            t0 = T.get([128, 12, 16], F32)
            numt = T.get([128, 4, 16], F32)
            dent = T.get([128, 4, 16], F32)
            Bt = Buf()
            qTs_, kTs_, vTs_ = sm[:, 0], sm[:, 1], sm[:, 2]
            for g in range(3):
                pq, Bpq = PS.one()
                for h in range(4):
                    tr(pq[:16, h * 128:(h + 1) * 128], qTs_[:, g * 4 + h, :], ident_f, adds=[Bpq])
                cp("act", qtm[g], pq[:16, :], reads=[Bpq], writes=[Bq[g]])
            pnum, Bnum = bank_ap(0), bankB[0]
            pden, Bden = bank_ap(1), bankB[1]
            state = {"first": True}

            def s1(g):
                d = DILS[g]
                L = 128 * d
                kvt, Bkv = kvg_t[g % 2], kvg_B[g % 2]
                for s in range(NSMP):
                    P.dma("pool", kvt[:, s, :], caches[g][s, 0:L:d, :], writes=[Bkv[s]])
                for s in range(NSMP):
                    pb, Bpb = PS.one()
                    mm(pb, sel[:, s, :], qtm[g], True, True, reads=[Bsel, Bq[g]], adds=[Bpb])
                    pr, Bpr = prr.next()
                    tt("dve", pr, kvt[:, s, 0:512], pb, ALU.mult, reads=[Bkv[s], Bpb], writes=[Bpr])
                    P.op("dve", lambda e, o=Sc[:, g, s, :], i=pr.rearrange("p (h x) -> p h x", h=4):
                         e.tensor_reduce(out=o, in_=i, axis=AX.X, op=ALU.add), reads=[Bpr], adds=[BSc[g]])
                act(E[:, g].rearrange("p s h -> p (s h)"), Sc[:, g].rearrange("p s h -> p (s h)"), AF.Exp,
                    reads=[BSc[g]], writes=[BE[g]], scale=SC_A)
                return kvt, Bkv

            def s2(g, kvt, Bkv):
                for s in range(NSMP):
                    for h in range(4):
                        mm(pnum[:, h * 16 + s:h * 16 + s + 1], kvt[:, s, 512 + h * 128:512 + (h + 1) * 128],
                           E[:, g, s, h:h + 1], state["first"], False, reads=[Bkv[s], BE[g]], adds=[Bnum])
                        state["first"] = False
                mm(pden[:, 0:64], ones_b, E[:, g].rearrange("p s h -> p (s h)"), g == 0, g == 2,
                   reads=[BE[g]], adds=[Bden])

            k0 = s1(0)
            k1 = s1(1)
            s2(0, *k0)
            k2 = s1(2)
            s2(1, *k1)
            s2(2, *k2)
            tt("dve", t0, qTs_, kTs_, ALU.mult, writes=[Bt])
            ps0, Bps0 = PS.one()
            mm(ps0[:, 0:192], ones_f, t0.rearrange("p a s -> p (a s)"), True, True, reads=[Bt], adds=[Bps0])
            act(e0, ps0[:, 0:192].rearrange("p (a s) -> p a s", s=16), AF.Exp, reads=[Bps0], writes=[Bt], scale=SC_A)
            tt("dve", t0, e0, vTs_, ALU.mult, reads=[Bt], writes=[Bt])
            tt("dve", numt, pnum[:, 0:64].rearrange("p (h s) -> p h s", s=16), t0[:, 0:4, :], ALU.add,
               reads=[Bnum, Bt], writes=[Bt])
            tt("dve", dent, pden[:, 0:64].rearrange("p (s h) -> p h s", h=4), e0[:, 0:4, :], ALU.add,
               reads=[Bden, Bt], writes=[Bt])
            for g in (1, 2):
                tt("dve", numt, numt, t0[:, g * 4:(g + 1) * 4, :], ALU.add, reads=[Bt], writes=[Bt])
                tt("dve", dent, dent, e0[:, g * 4:(g + 1) * 4, :], ALU.add, reads=[Bt], writes=[Bt])
            recip(dent, dent, reads=[Bt], writes=[Bt])
            tt("dve", attT[:, :, SEQ:NTOK], numt, dent, ALU.mult, reads=[Bt])
            P.barrier()

        def phaseB2():
            PS.set([0, 1, 2, 3], [2, 3])
            T = A.sub(X3, TOTAL)
            Wq = T.get([128, 8, 512], BF16)
            Wk = T.get([128, 8, 512], BF16)
            Wv = T.get([128, 8, 1024], BF16)
            Wg = T.get([128, 8, 1024], BF16)
            BW = Buf()
            P.dma("pool", Wq, w_in[:, OFF_QR:OFF_QR + 512].rearrange("(kc p) n -> p kc n", p=128), adds=[BW])
            P.dma("pool", Wk, w_in[:, OFF_KR:OFF_KR + 512].rearrange("(kc p) n -> p kc n", p=128), adds=[BW])
            for hf in range(2):
                P.dma("pool", Wv[:, :, hf * 512:(hf + 1) * 512],
                      w_in[:, OFF_VR + hf * 512:OFF_VR + (hf + 1) * 512].rearrange("(kc p) n -> p kc n", p=128), adds=[BW])
                P.dma("pool", Wg[:, :, hf * 512:(hf + 1) * 512],
                      w_in[:, OFF_GR + hf * 512:OFF_GR + (hf + 1) * 512].rearrange("(kc p) n -> p kc n", p=128), adds=[BW])
            cos = T.get([128, 17, 64], F32)
            sin = T.get([128, 17, 64], F32)
            dtb = T.get([128, 4, 128], F32)
            small = T.get([128, 8], F32)
            gret = T.get([128, 1024], F32)
            i16s = T.get([16, 16], F32)
            i16bc = T.get([128, 16, 16], F32)
            Bk = Buf()
            P.dma("sp", cos, k_cos.rearrange("p (t j) -> p t j", j=64), adds=[Bk])
            P.dma("sp", sin, k_sin.rearrange("p (t j) -> p t j", j=64), adds=[Bk])
            P.dma("sp", dtb, k_dt.rearrange("p (h t) -> p h t", t=128), adds=[Bk])
            P.dma("sp", small, k_small, adds=[Bk])
            P.dma("sp", gret, k_grep[:, 2 * D:3 * D], adds=[Bk])
            P.dma("sp", i16s, k_i16s, adds=[Bk])
            P.dma("sp", i16bc, k_i16bc.rearrange("p (s j) -> p s j", j=16), adds=[Bk])
            S = T.get([128, 4, 256], F32)
            Sbf = T.get([128, 4, 256], BF16)
            BS = [Buf() for _ in range(4)]
            BSb = [Buf() for _ in range(4)]
            tmpr = Ring([T.get([128, 4, 64], F32) for _ in range(4)])
            sgr = Ring([T.get([128, 1024], F32) for _ in range(2)])
            rnr = Ring([T.get([128, 1024], F32) for _ in range(1)])
            rgr = Ring([T.get([128, 1024], BF16) for _ in range(2)])
            str_ = Ring([T.get([128, 4, 8], F32) for _ in range(2)])
            rsr = Ring([T.get([128, 8], F32) for _ in range(2)])
            LA = T.cur

            def rope(psrc, Bp, np_, t, dst, Bdst):
                xv = psrc.rearrange("p (h a j) -> p h a j", h=4, a=2)
                dv = dst.rearrange("p (h a j) -> p h a j", h=4, a=2)
                cb = cos[:np_, t, :].unsqueeze(1).to_broadcast([np_, 4, 64])
                sb = sin[:np_, t, :].unsqueeze(1).to_broadcast([np_, 4, 64])
                ta, Ba = tmpr.next()
                tb, Bb = tmpr.next()
                tt("dve", ta[:np_], xv[:, :, 0, :], cb, ALU.mult, reads=[Bp, Bk], writes=[Ba])
                tt("dve", tb[:np_], xv[:, :, 1, :], sb, ALU.mult, reads=[Bp, Bk], writes=[Bb])
                tt("pool", dv[:, :, 0, :], ta[:np_], tb[:np_], ALU.subtract, reads=[Ba, Bb], adds=[Bdst])
                tc_, Bc_ = tmpr.next()
                td, Bd = tmpr.next()
                tt("dve", tc_[:np_], xv[:, :, 0, :], sb, ALU.mult, reads=[Bp, Bk], writes=[Bc_])
                tt("dve", td[:np_], xv[:, :, 1, :], cb, ALU.mult, reads=[Bp, Bk], writes=[Bd])
                tt("pool", dv[:, :, 1, :], tc_[:np_], td[:np_], ALU.add, reads=[Bc_, Bd], adds=[Bdst])

            def groupnorm_gate_T(pO, BpO, sg, Bsg, np_, tok0, defer=False):
                st6, Bst = str_.next()
                for h in range(4):
                    P.op("dve", lambda e, o=st6[:np_, h, 0:6], i=pO[:, h * 256:(h + 1) * 256]: e.bn_stats(out=o, in_=i),
                         reads=BpO, adds=[Bst])
                for h in range(4):
                    P.op("dve", lambda e, o=st6[:np_, h, 6:8], i=st6[:np_, h, 0:6]: e.bn_aggr(out=o, in_=i),
                         reads=[Bst], writes=[Bst])
                rs, Brs = rsr.next()
                act(rs[:np_, 0:4], st6[:np_, :, 7], AF.Sqrt, reads=[Bst], writes=[Brs], bias=EPS)
                recip(rs[:np_, 4:8], rs[:np_, 0:4], reads=[Brs], writes=[Brs])
                rn, Brn = rnr.next()
                for h in range(4):
                    ts("dve", rn[:np_, h * 256:(h + 1) * 256], pO[:, h * 256:(h + 1) * 256], st6[:np_, h, 6:7],
                       rs[:np_, 4 + h:5 + h], ALU.subtract, ALU.mult, reads=list(BpO) + [Bst, Brs],
                       writes=([Brn] if h == 0 else ()), adds=(() if h == 0 else [Brn]))
                tt("dve", rn[:np_], rn[:np_], gret[:np_], ALU.mult, reads=[Brn, Bk], writes=[Brn])
                rg, Brg = rgr.next()
                tt("pool", rg[:np_], rn[:np_], sg[:np_], ALU.mult, reads=[Brn, Bsg], writes=[Brg])

                def fin():
                    pT, BpT = PS.one()
                    pTv = pT.bitcast(BF16).rearrange("p (a b) -> p a b", b=128)
                    for kc in range(8):
                        tr(pTv[:, kc, :np_], rg[:np_, kc * 128:(kc + 1) * 128], ident_b[:np_, :np_], reads=[Brg], adds=[BpT])
                    cp("act", rgT[:, :, tok0:tok0 + np_], pTv[:, :, :np_], reads=[BpT])
                if defer:
                    return fin
                fin()

            def proj_tm(W, ncol, np_, tok):
                if ncol == 512:
                    p_, Bp_ = PS.one()
                    for kc in range(8):
                        mm(p_[:np_], hT[:, kc, tok], W[:, kc, :], kc == 0, kc == 7, reads=[BW], adds=[Bp_])
                    return p_[:np_], [Bp_]
                p_, Bp_ = PS.two()
                for hf in range(2):
                    for kc in range(8):
                        mm(p_[:np_, hf * 512:(hf + 1) * 512], hT[:, kc, tok], W[:, kc, hf * 512:(hf + 1) * 512],
                           kc == 0, kc == 7, reads=[BW], adds=[Bp_[hf]])
                return p_[:np_], Bp_

            L1 = Arena(ar_t, LA, TOTAL)
            qrr = Ring([L1.get([128, 512], BF16) for _ in range(2)])
            krr = Ring([L1.get([128, 512], BF16) for _ in range(2)])
            qpr = Ring([L1.get([128, 512], BF16) for _ in range(2)])
            kpr = Ring([L1.get([128, 512], BF16) for _ in range(2)])
            qTr = Ring([L1.get([128, 8, 128], BF16) for _ in range(2)])
            kTr = Ring([L1.get([128, 4, 128], BF16) for _ in range(2)])
            vrr = Ring([L1.get([128, 1024], BF16) for _ in range(2)])
            stm = Ring([L1.get([128, 128], BF16) for _ in range(5)])
            def b2_s1(t):
                tok = slice(t * 128, (t + 1) * 128)
                pq, Bpq = proj_tm(Wq, 512, 128, tok)
                qr, Bqr = qrr.next()
                rope(pq, Bpq[0], 128, t, qr, Bqr)
                pk, Bpk = proj_tm(Wk, 512, 128, tok)
                kr, Bkr = krr.next()
                rope(pk, Bpk[0], 128, t, kr, Bkr)
                pv, Bpv = proj_tm(Wv, 1024, 128, tok)
                vr, Bvr = vrr.next()
                cp("act", vr, pv, reads=Bpv, writes=[Bvr])
                pg, Bpg = proj_tm(Wg, 1024, 128, tok)
                sg, Bsg = sgr.next()
                act(sg, pg, AF.Silu, reads=Bpg, writes=[Bsg])
                qp, Bqp = qpr.next()
                kp, Bkp = kpr.next()
                tt("pool", qp.rearrange("p (h x) -> p h x", h=4), qr.rearrange("p (h x) -> p h x", h=4),
                   small[:, 0:4].unsqueeze(2).to_broadcast([128, 4, 128]), ALU.mult, reads=[Bqr, Bk], writes=[Bqp])
                tt("pool", kp.rearrange("p (h x) -> p h x", h=4), kr.rearrange("p (h x) -> p h x", h=4),
                   small[:, 4:8].unsqueeze(2).to_broadcast([128, 4, 128]), ALU.mult, reads=[Bkr, Bk], writes=[Bkp])
                return dict(t=t, qr=qr, Bqr=Bqr, kr=kr, Bkr=Bkr, vr=vr, Bvr=Bvr, sg=sg, Bsg=Bsg,
                            qp=qp, Bqp=Bqp, kp=kp, Bkp=Bkp)

            def b2_t(c):
                qr, Bqr, kr, Bkr = c["qr"], c["Bqr"], c["kr"], c["Bkr"]
                qp, Bqp = c["qp"], c["Bqp"]
                pT1, BpT1 = PS.one()
                pT1v = pT1.bitcast(BF16).rearrange("p (a b) -> p a b", b=128)
                for h in range(4):
                    tr(pT1v[:, h, :], qr[:, h * 128:(h + 1) * 128], ident_b, reads=[Bqr], adds=[BpT1])
                for h in range(4):
                    tr(pT1v[:, 4 + h, :], qp[:, h * 128:(h + 1) * 128], ident_b, reads=[Bqp], adds=[BpT1])
                qT, BqT = qTr.next()
                cp("act", qT, pT1v, reads=[BpT1], writes=[BqT])
                pT2, BpT2 = PS.one()
                pT2v = pT2.bitcast(BF16).rearrange("p (a b) -> p a b", b=128)
                for h in range(4):
                    tr(pT2v[:, h, :], kr[:, h * 128:(h + 1) * 128], ident_b, reads=[Bkr], adds=[BpT2])
                kT, BkT = kTr.next()
                cp("dve", kT, pT2v[:, 0:4, :], reads=[BpT2], writes=[BkT])
                c["qT"], c["BqT"], c["kT"], c["BkT"] = qT, BqT, kT, BkT

            def b2_s2(c):
                t = c["t"]
                vr, Bvr, kp, Bkp = c["vr"], c["Bvr"], c["kp"], c["Bkp"]
                qT, BqT, kT, BkT = c["qT"], c["BqT"], c["kT"], c["BkT"]
                pO, BpO = PS.two()
                pSU, BpSU = PS.two()
                sTs = []
                for h in range(4):
                    psT, BpsT = PS.one()
                    mm(psT[:, 0:128], kT[:, h, :], qT[:, h, :], True, True, reads=[BkT, BqT], adds=[BpsT])
                    sT, BsT = stm.next()
                    tt("dve", sT, psT[:, 0:128], dtb[:, h, :], ALU.mult, reads=[BpsT, Bk], writes=[BsT])
                    sTs.append((sT, BsT))
                for h in range(4):
                    oh = pO[:, h * 256:(h + 1) * 256]
                    if t > 0:
                        mm(oh, qT[:, 4 + h, :], Sbf[:, h, :], h % 2 == 0, False, reads=[BqT, BSb[h]], adds=[BpO[h // 2]])
                for h in range(4):
                    sT, BsT = sTs[h]
                    oh = pO[:, h * 256:(h + 1) * 256]
                    mm(oh, sT, vr[:, h * 256:(h + 1) * 256], (t == 0 and h % 2 == 0), True, reads=[BsT, Bvr],
                       adds=[BpO[h // 2]])
                for h in range(4):
                    su = pSU[:, h * 256:(h + 1) * 256]
                    mm(su, kp[:, h * 128:(h + 1) * 128], vr[:, h * 256:(h + 1) * 256], h % 2 == 0, True,
                       reads=[Bkp, Bvr], adds=[BpSU[h // 2]])
                for h in range(4):
                    su = pSU[:, h * 256:(h + 1) * 256]
                    if t == 0:
                        cp("dve", S[:, h, :], su, reads=[BpSU[h // 2]], writes=[BS[h]])
                    else:
                        stt(S[:, h, :], S[:, h, :], GAM[h] ** 128, su, ALU.mult, ALU.add,
                            reads=[BpSU[h // 2], BS[h]], writes=[BS[h]])
                    if t < 15:
                        cp("act", Sbf[:, h, :], S[:, h, :], reads=[BS[h]], writes=[BSb[h]])
                c["pO"], c["BpO"] = pO, BpO

            ctx = {}
            ctx[0] = b2_s1(0)
            prev3 = None
            for t in range(16):
                b2_t(ctx[t])
                if t + 1 < 16:
                    ctx[t + 1] = b2_s1(t + 1)
                b2_s2(ctx[t])
                c = ctx[t]
                g3 = groupnorm_gate_T(c["pO"], c["BpO"], c["sg"], c["Bsg"], 128, t * 128, defer=True)
                if prev3 is not None:
                    prev3()
                prev3 = g3
            prev3()
            P.dma("sp", stp.rearrange("h p v -> p h v"), S, reads=BS)
            P.barrier()

            L2 = Arena(ar_t, LA, TOTAL)
            qr = L2.get([16, 512], F32)
            kr = L2.get([16, 512], F32)
            vrs = L2.get([16, 1024], BF16)
            qT2 = L2.get([128, 4, 16], F32)
            str2 = Ring([L2.get([128, 4, 256], F32) for _ in range(3)])
            snr = Ring([L2.get([128, 4, 256], F32) for _ in range(2)])
            vsr = Ring([L2.get([16, 512], BF16) for _ in range(2)])
            qsr = Ring([L2.get([128, 4, 16], BF16) for _ in range(2)])
            snbr = Ring([L2.get([128, 4, 256], BF16) for _ in range(2)])
            Bqr, Bkr, Bvrs, BqT2 = Buf(), Buf(), Buf(), Buf()
            tok = slice(SEQ, NTOK)
            pq, Bpq = proj_tm(Wq, 512, 16, tok)
            rope(pq, Bpq[0], 16, 16, qr, Bqr)
            pk, Bpk = proj_tm(Wk, 512, 16, tok)
            rope(pk, Bpk[0], 16, 16, kr, Bkr)
            pv, Bpv = proj_tm(Wv, 1024, 16, tok)
            cp("act", vrs, pv, reads=Bpv, writes=[Bvrs])
            pg, Bpg = proj_tm(Wg, 1024, 16, tok)
            sg, Bsg = sgr.next()
            act(sg[:16], pg, AF.Silu, reads=Bpg, writes=[Bsg])
            pTq, BpTq = PS.one()
            for h in range(4):
                tr(pTq[:, h * 16:(h + 1) * 16], qr[:, h * 128:(h + 1) * 128], ident_f[:16, :16], reads=[Bqr], adds=[BpTq])
            cp("act", qT2, pTq[:, 0:64].rearrange("p (h s) -> p h s", s=16), reads=[BpTq], writes=[BqT2])
            pOs, BpOs = PS.two()
            Sts = {}

            def st_load(s_):
                St_, BSt_ = str2.next()
                P.dma("sp", St_, st_in[s_].rearrange("h p v -> p h v"), writes=[BSt_])
                Sts[s_] = (St_, BSt_)
            for s_ in range(3):
                st_load(s_)
            for s in range(NSMP):
                vs, Bvs = vsr.next()
                ts("dve", vs, kr, i16s[:, s:s + 1], None, ALU.mult, reads=[Bkr, Bk], writes=[Bvs])
                St, BSt = Sts.pop(s)
                Sn, BSn = snr.next()
                for hh in range(2):
                    po, Bpo = PS.one()
                    for h2 in range(2):
                        h = hh * 2 + h2
                        mm(po[:, h2 * 256:(h2 + 1) * 256], vs[:, h * 128:(h + 1) * 128],
                           vrs[:, h * 256:(h + 1) * 256], h2 == 0, h2 == 1, reads=[Bvrs, Bvs], adds=[Bpo])
                    for h2 in range(2):
                        h = hh * 2 + h2
                        stt(Sn[:, h, :], St[:, h, :], GAM[h], po[:, h2 * 256:(h2 + 1) * 256], ALU.mult, ALU.add,
                            reads=[BSt, Bpo], adds=[BSn])
                P.dma("act", sts[s].rearrange("h p v -> p h v"), Sn, reads=[BSn])
                if s + 3 < NSMP:
                    st_load(s + 3)
                qs, Bqs = qsr.next()
                tt("dve", qs, qT2[:, :, s:s + 1].to_broadcast([128, 4, 16]),
                   i16bc[:, s, :].unsqueeze(1).to_broadcast([128, 4, 16]), ALU.mult, reads=[BqT2, Bk], writes=[Bqs])
                Snb, BSnb = snbr.next()
                cp("act", Snb, Sn, reads=[BSn], writes=[BSnb])
                for h in range(4):
                    mm(pOs[:16, h * 256:(h + 1) * 256], qs[:, h, :], Snb[:, h, :], (s == 0 and h % 2 == 0),
                       (s == NSMP - 1), reads=[Bqs, BSnb], adds=[BpOs[h // 2]])
            groupnorm_gate_T(pOs[:16], BpOs, sg, Bsg, 16, SEQ)
            P.barrier()

        def phaseC1():
            PS.set(list(range(8)), [0])
            T = A.sub(X4, TOTAL)
            cr = Ring([T.get([128, 8, 128], BF16) for _ in range(8)])
            sgr_ = Ring([T.get([128, 512], F32) for _ in range(4)])
            m1r = Ring([T.get([128, 512], F32) for _ in range(4)])
            for c in range(8):
                cs = slice(c * 128, (c + 1) * 128)
                Wa, BWa = cr.next()
                wload(Wa[:, 0:4, :], w_att[:, cs], [BWa])
                Wr, BWr = cr.next()
                wload(Wr, w_ret[:, cs], [BWr])
                Wga, BWga = cr.next()
                wload(Wga, w_in[:, OFF_GA + c * 128:OFF_GA + (c + 1) * 128], [BWga])
                Wgm, BWgm = cr.next()
                wload(Wgm, w_in[:, OFF_GM + c * 128:OFF_GM + (c + 1) * 128], [BWgm])
                for (t0_, n) in TB:
                    tk = slice(t0_, t0_ + n)
                    pa, Bpa = PS.one()
                    for kc in range(4):
                        mm(pa[:, :n], Wa[:, kc, :], attT[:, kc, tk], kc == 0, kc == 3, reads=[BWa], adds=[Bpa])
                    pga, Bpga = PS.one()
                    for kc in range(8):
                        mm(pga[:, :n], Wga[:, kc, :], hT[:, kc, tk], kc == 0, kc == 7, reads=[BWga], adds=[Bpga])
                    sga, Bsga = sgr_.next()
                    act(sga[:, :n], pga[:, :n], AF.Sigmoid, reads=[Bpga], writes=[Bsga])
                    m1, Bm1 = m1r.next()
                    tt("dve", m1[:, :n], pa[:, :n], sga[:, :n], ALU.mult, reads=[Bpa, Bsga], writes=[Bm1])
                    pr, Bpr = PS.one()
                    for kc in range(8):
                        mm(pr[:, :n], Wr[:, kc, :], rgT[:, kc, tk], kc == 0, kc == 7, reads=[BWr], adds=[Bpr])
                    pgm, Bpgm = PS.one()
                    for kc in range(8):
                        mm(pgm[:, :n], Wgm[:, kc, :], hT[:, kc, tk], kc == 0, kc == 7, reads=[BWgm], adds=[Bpgm])
                    sgm, Bsgm = sgr_.next()
                    act(sgm[:, :n], pgm[:, :n], AF.Sigmoid, reads=[Bpgm], writes=[Bsgm])
                    m2, Bm2 = m1r.next()
                    tt("dve", m2[:, :n], pr[:, :n], sgm[:, :n], ALU.mult, reads=[Bpr, Bsgm], writes=[Bm2])
                    tt("dve", mT[:, c, tk], m1[:, :n], m2[:, :n], ALU.add, reads=[Bm1, Bm2])
            P.barrier()

        def phaseC2():
            PS.set([0, 1, 2, 3], [2, 3])
            T1 = A.sub(X1, X3)
            T2 = A.sub(X4, TOTAL)
            Wo = T1.get([128, 8, 1024], BF16)
            BWo = Buf()
            for hf in range(2):
                P.dma("pool", Wo[:, :, hf * 512:(hf + 1) * 512],
                      w_out[:, hf * 512:(hf + 1) * 512].rearrange("(kc p) n -> p kc n", p=128), adds=[BWo])
            gpost = T1.get([128, 1024], F32)
            Bg = Buf()
            P.dma("sp", gpost, k_grep[:, 0:D], writes=[Bg])
            R = {"stat": Ring([T1.get([128, 4], F32) for _ in range(6)]),
                 "junk": T1.get([128, 1024], F32),
                 "hb": Ring([T1.get([128, 1024], BF16) for _ in range(3)])}
            xr = Ring([T2.get([128, 1024], F32) for _ in range(4)])
            tmr = Ring([T2.get([128, 1024], F32) for _ in range(2)])
            x1r = Ring([T2.get([128, 1024], F32) for _ in range(4)])
            def c2_s1(t):
                np_ = 128 if t < 16 else 16
                tok = slice(t * 128, t * 128 + np_)
                src = xp[t * 128:(t + 1) * 128, :] if t < 16 else xs
                xt, Bx = xr.next()
                P.dma("sp", xt[:np_], src, writes=[Bx])
                po, Bpo = PS.two()
                for hf in range(2):
                    for kc in range(8):
                        mm(po[:np_, hf * 512:(hf + 1) * 512], mT[:, kc, tok], Wo[:, kc, hf * 512:(hf + 1) * 512],
                           kc == 0, kc == 7, reads=[BWo], adds=[Bpo[hf]])
                return (t, np_, tok, xt, Bx, po, Bpo)

            def c2_s2a(t, np_, tok, xt, Bx, po, Bpo):
                rstd, Bs = rms_rstd(po[:np_], np_, R["stat"], R["junk"], Bpo)
                tm, Btm = tmr.next()
                stt(tm[:np_], po[:np_], rstd, gpost[:np_], ALU.mult, ALU.mult, reads=list(Bpo) + [Bs, Bg], writes=[Btm])
                x1, Bx1 = x1r.next()
                tt("pool", x1[:np_], xt[:np_], tm[:np_], ALU.add, reads=[Bx, Btm], writes=[Bx1])
                P.dma("sp", x1d[tok, :], x1[:np_], reads=[Bx1])
                return (t, np_, x1, Bx1)

            def c2_s2b(t, np_, x1, Bx1):
                hb, Bhb = norm_p1(x1[:np_], np_, R, [Bx1])
                return (hb, Bhb, np_, t * 128, 8, h2T)

            NT_ = 17
            st1, st2a, st2b = {}, {}, {}
            for i in range(NT_ + 3):
                if i < NT_:
                    st1[i] = c2_s1(i)
                if 0 <= i - 1 < NT_:
                    st2a[i - 1] = c2_s2a(*st1.pop(i - 1))
                if 0 <= i - 2 < NT_:
                    st2b[i - 2] = c2_s2b(*st2a.pop(i - 2))
                if 0 <= i - 3 < NT_:
                    norm_p2(*st2b.pop(i - 3))
            P.barrier()

        def phaseD():
            PS.set([0, 1, 2, 3], [2, 3])
            T = A.sub(X1, TOTAL)
            Wd = T.get([128, 32, 1024], BF16)
            BWd = Buf()
            wr = Ring([T.get([128, 8, 512], BF16) for _ in range(3)])
            Wu0, BWu0 = wr.next()
            wload(Wu0, w_up[:, 0:512], [BWu0])
            def wd_piece(i_):
                q4, hf = i_ // 2, i_ % 2
                P.dma("pool", Wd[:, q4 * 8:(q4 + 1) * 8, hf * 512:(hf + 1) * 512],
                      w_down[q4 * 1024:(q4 + 1) * 1024, hf * 512:(hf + 1) * 512].rearrange("(kc p) n -> p kc n", p=128),
                      adds=[BWd])
            gpost = T.get([128, 1024], F32)
            Bg = Buf()
            P.dma("sp", gpost, k_grep[:, D:2 * D], writes=[Bg])
            uT = T.get([128, 32, 768], BF16)
            BuT = [Buf() for _ in range(32)]
            rr = Ring([T.get([128, 512], F32) for _ in range(3)])
            statr = Ring([T.get([128, 4], F32) for _ in range(4)])
            junk = T.get([128, 1024], F32)
            tmr = Ring([T.get([128, 1024], F32) for _ in range(1)])
            x1r = Ring([T.get([128, 1024], F32) for _ in range(2)])
            yr = Ring([T.get([128, 1024], F32) for _ in range(2)])
            groups = [[(0, 512), (512, 256)], [(768, 512), (1280, 256)], [(1536, 512), (2048, 16)]]
            for gi, grp in enumerate(groups):
                for fb in range(8):
                    if gi == 0 and fb == 0:
                        Wu, BWu = Wu0, BWu0
                    else:
                        Wu, BWu = wr.next()
                        wload(Wu, w_up[:, fb * 512:(fb + 1) * 512], [BWu])
                    if gi == 0:
                        wd_piece(fb)
                    for j in range(4):
                        f = fb * 4 + j
                        off = 0
                        for (t0_, n) in grp:
                            pu, Bpu = PS.one()
                            for kc in range(8):
                                mm(pu[:, :n], Wu[:, kc, j * 128:(j + 1) * 128], h2T[:, kc, t0_:t0_ + n], kc == 0, kc == 7,
                                   reads=[BWu], adds=[Bpu])
                            r, Br = rr.next()
                            act(r[:, :n], pu[:, :n], AF.Relu, reads=[Bpu], writes=[Br])
                            tt("dve", uT[:, f, off:off + n], r[:, :n], r[:, :n], ALU.mult, reads=[Br], adds=[BuT[f]])
                            off += n
                tiles = []
                off = 0
                for (t0_, n) in grp:
                    for i in range(0, n, 128):
                        tiles.append((t0_ + i, off + i, min(128, n - i)))
                    off += n
                for (tok0, loc, np_) in tiles:
                    x1, Bx1 = x1r.next()
                    P.dma("sp", x1[:np_], x1d[tok0:tok0 + np_, :], writes=[Bx1])
                    py, Bpy = PS.two()
                    for hf in range(2):
                        for f in range(32):
                            mm(py[:np_, hf * 512:(hf + 1) * 512], uT[:, f, loc:loc + np_], Wd[:, f, hf * 512:(hf + 1) * 512],
                               f == 0, f == 31, reads=[BuT[f], BWd], adds=[Bpy[hf]])
                    rstd, Bs = rms_rstd(py[:np_], np_, statr, junk, Bpy)
                    tm, Btm = tmr.next()
                    stt(tm[:np_], py[:np_], rstd, gpost[:np_], ALU.mult, ALU.mult, reads=list(Bpy) + [Bs, Bg], writes=[Btm])
                    y, By = yr.next()
                    tt("dve", y[:np_], x1[:np_], tm[:np_], ALU.add, reads=[Bx1, Btm], writes=[By])
                    dst = yp[tok0:tok0 + np_, :] if tok0 < SEQ else ys
                    P.dma("sp", dst, y[:np_], reads=[By])

        genB1 = phaseB1()
        sm = phaseA(genB1)
        phaseB1c(sm)
        phaseB2()
        phaseC1()
        phaseC2()
        phaseD()
        P.emit()
    return nc


def _constants():
    f32 = np.float32
    c = {}
    c["k_ident"] = np.eye(128, dtype=f32)
    ip = np.arange(128)[:, None]
    iq = np.arange(128)[None, :]
    mprev = np.where(ip >= iq, 0.0, NEG)
    mown = np.where(ip <= iq, 0.0, NEG)
    c["k_maskb"] = np.concatenate([mprev, mown], axis=1).astype(f32)
    half = 64
    inv = (f32(10000.0) ** (-(np.arange(half, dtype=f32) / f32(half)))).astype(f32)
    pos = np.zeros((128, 17), dtype=f32)
    for t in range(16):
        pos[:, t] = t * 128 + np.arange(128)
    pos[:, 16] = PAST
    ang = (pos[:, :, None] * inv[None, None, :]).astype(f32)
    c["k_cos"] = np.cos(ang).astype(f32).reshape(128, 17 * 64)
    c["k_sin"] = np.sin(ang).astype(f32).reshape(128, 17 * 64)
    g = np.array(GAM, dtype=np.float64)
    tq = np.arange(128)
    rel = tq[None, :] - tq[:, None]
    dt = np.zeros((128, 4, 128), dtype=np.float64)
    for h in range(4):
        dt[:, h, :] = np.where(rel >= 0, g[h] ** np.maximum(rel, 0), 0.0) * SC_A
    c["k_dt"] = dt.reshape(128, 512).astype(f32)
    small = np.zeros((128, 8), dtype=np.float64)
    for h in range(4):
        small[:, h] = g[h] ** (tq + 1)
        small[:, 4 + h] = SC_A * g[h] ** (127 - tq)
    c["k_small"] = small.astype(f32)
    sel = np.zeros((16, 16, 128), dtype=f32)
    for s in range(16):
        sel[s, s, :] = 1.0
    c["k_sel"] = sel.reshape(16, 2048)
    c["k_i16s"] = (np.eye(16) * SC_A).astype(f32)
    bc = np.zeros((128, 16, 16), dtype=f32)
    for s in range(16):
        bc[:, s, s] = 1.0
    c["k_i16bc"] = bc.reshape(128, 256)
    return c


_NC_CACHE = {}


def kernel(x_prompt, x_sample, cache_kv_d1, cache_kv_d4, cache_kv_d16, state_ret,
           w_in, w_att_br, w_ret_br, w_out, w_up, w_down,
           g_ret_norm, g_pre_mix, g_post_mix, g_pre_mlp, g_post_mlp):
    f32 = np.float32
    A_ = lambda a: np.ascontiguousarray(np.asarray(a, dtype=f32))
    if "nc" not in _NC_CACHE:
        _NC_CACHE["nc"] = build_program()
    nc = _NC_CACHE["nc"]
    consts = _constants()
    gpre = np.concatenate([A_(g_pre_mix)[0].reshape(8, 128).T, A_(g_pre_mlp)[0].reshape(8, 128).T], axis=1)
    grep = np.concatenate([np.broadcast_to(A_(g_post_mix)[0], (128, D)), np.broadcast_to(A_(g_post_mlp)[0], (128, D)),
                           np.broadcast_to(A_(g_ret_norm)[0], (128, D))], axis=1)
    shared = dict(consts)
    shared.update({
        "w_in": A_(w_in)[0], "w_att": A_(w_att_br)[0], "w_ret": A_(w_ret_br)[0], "w_out": A_(w_out)[0],
        "w_up": A_(w_up)[0], "w_down": A_(w_down)[0],
        "k_gpre": np.ascontiguousarray(gpre, dtype=f32), "k_grep": np.ascontiguousarray(grep, dtype=f32),
    })
    xpr, xsm = A_(x_prompt), A_(x_sample)
    c1, c4, c16, stt_ = A_(cache_kv_d1)[0], A_(cache_kv_d4)[0], A_(cache_kv_d16)[0], A_(state_ret)[0]
    in_maps = []
    for i in range(8):
        sl = slice(i * NSMP, (i + 1) * NSMP)
        m = dict(shared)
        m["xp"] = xpr[i]
        m["xs"] = xsm[sl, 0, :]
        m["c1"] = c1[sl].reshape(NSMP, 128, 1024)
        m["c4"] = c4[sl].reshape(NSMP, 512, 1024)
        m["c16"] = c16[sl].reshape(NSMP, 2048, 1024)
        m["st"] = stt_[sl]
        in_maps.append(m)
    res = run_bass_kernel_spmd(nc, in_maps, core_ids=list(range(8)))
    R = res.results
    cat = lambda name: np.stack([np.asarray(r[name]) for r in R], axis=0)
    y_p = cat("yp")
    y_s = np.concatenate([np.asarray(r["ys"]) for r in R], axis=0).reshape(128, 1, D)
    kv1p = cat("kv1p").reshape(1, 8, 128, 2, 4, 128)
    kv4p = cat("kv4p").reshape(1, 8, 512, 2, 4, 128)
    kv16p = cat("kv16p").reshape(1, 8, 2048, 2, 4, 128)
    stp_ = cat("stp").reshape(1, 8, 4, 128, 256)
    ccat = lambda name: np.concatenate([np.asarray(r[name]) for r in R], axis=0)
    kv1s = ccat("kv1s").reshape(1, 128, 1, 2, 4, 128)
    kv4s = ccat("kv4s").reshape(1, 128, 1, 2, 4, 128)
    kv16s = ccat("kv16s").reshape(1, 128, 1, 2, 4, 128)
    sts_ = ccat("sts").reshape(1, 128, 4, 128, 256)
    return (y_p.astype(f32), y_s.astype(f32), kv1p, kv4p, kv16p, stp_, kv1s, kv4s, kv16s, sts_)
```

```python
import numpy as np
from contextlib import ExitStack
import concourse.bass as bass
import concourse.mybir as mybir
from concourse.bass_utils import run_bass_kernel_spmd

F32 = mybir.dt.float32
BF16 = mybir.dt.bfloat16
F32R = mybir.dt.float32r
AF = mybir.ActivationFunctionType
ALU = mybir.AluOpType
AX = mybir.AxisListType

D = 1024
SEQ = 2048
NSMP = 16
NTOK = SEQ + NSMP
PAST = 8192
OFF_QA, OFF_KA, OFF_VA = 0, 1536, 3072
OFF_QR, OFF_KR, OFF_VR, OFF_GR = 4608, 5120, 5632, 6656
OFF_GA, OFF_GM = 7680, 8704
IN_COLS = 9728
EPS = 1e-6
NEG = -30000.0
SC_A = 128 ** -0.5
GAM = [1.0 - 2.0 ** (-5 - h) for h in range(4)]
DILS = [1, 4, 16]
TB = [(0, 512), (512, 512), (1024, 512), (1536, 512), (2048, 16)]


class Buf:
    __slots__ = ("writers", "readers")

    def __init__(self):
        self.writers = set()
        self.readers = set()


class Prog:
    NDSEM = 24

    def __init__(self, nc):
        self.nc = nc
        self.order = ["pe", "act", "dve", "pool", "sp"]
        self.recs = {e: [] for e in self.order}
        self.ndma = 0
        self.bar = {}

    def _deps(self, eng, reads, writes, adds):
        deps = set()
        for b in reads:
            deps |= b.writers
        for b in writes:
            deps |= b.writers
            deps |= b.readers
        for b in adds:
            deps |= b.readers
        deps |= self.bar.pop(eng, set())
        return deps

    def _commit(self, tok, reads, writes, adds):
        for b in adds:
            if b.readers:
                b.writers = {tok}
                b.readers = set()
            else:
                b.writers.add(tok)
        for b in writes:
            b.writers = {tok}
            b.readers = set()
        for b in reads:
            b.readers.add(tok)

    def op(self, eng, fn, reads=(), writes=(), adds=()):
        deps = self._deps(eng, reads, writes, adds)
        idx = len(self.recs[eng])
        if eng == "pe":
            deps = {d for d in deps if not (d[0] == "e" and d[1] == "pe")}
        tok = ("e", eng, idx)
        self.recs[eng].append({"fn": fn, "deps": deps, "dma": None})
        self._commit(tok, reads, writes, adds)
        return tok

    def dma(self, eng, out, in_, reads=(), writes=(), adds=()):
        deps = self._deps(eng, reads, writes, adds)
        k = self.ndma
        self.ndma += 1
        if k >= self.NDSEM:
            deps.add(("d", k - self.NDSEM))
        tok = ("d", k)
        self.recs[eng].append({"fn": (lambda e: e.dma_start(out=out, in_=in_)), "deps": deps, "dma": k})
        self._commit(tok, reads, writes, adds)
        return tok

    def barrier(self):
        toks = set()
        for e in self.order:
            for i in range(len(self.recs[e]) - 1, -1, -1):
                if self.recs[e][i]["dma"] is None:
                    toks.add(("e", e, i))
                    break
        for k in range(max(0, self.ndma - self.NDSEM), self.ndma):
            toks.add(("d", k))
        for e in self.order:
            self.bar[e] = set(toks) | self.bar.get(e, set())

    def emit(self):
        nc = self.nc
        sig = {e: {} for e in self.order}
        for e in self.order:
            for r in self.recs[e]:
                for d in r["deps"]:
                    if d[0] == "e":
                        sig[d[1]][d[2]] = 0
        for e in self.order:
            c = 0
            for i in range(len(self.recs[e])):
                if i in sig[e]:
                    c += 1
                    sig[e][i] = c
        with ExitStack() as st:
            esem = {e: st.enter_context(nc.semaphore("es_" + e)) for e in self.order}
            dsem = [st.enter_context(nc.semaphore("ds%d" % i)) for i in range(self.NDSEM)]
            block = st.enter_context(nc.Block())
            ndma = self.ndma
            NS = self.NDSEM

            def body(e):
                def f(eng):
                    waited = {}
                    for i, r in enumerate(self.recs[e]):
                        need = {}
                        for d in r["deps"]:
                            if d[0] == "e":
                                key = ("e", d[1])
                                val = sig[d[1]][d[2]]
                            else:
                                key = ("d", d[1] % NS)
                                val = 16 * (d[1] // NS + 1)
                            if val > need.get(key, 0):
                                need[key] = val
                        for key, val in need.items():
                            if waited.get(key, 0) >= val:
                                continue
                            waited[key] = val
                            s = esem[key[1]] if key[0] == "e" else dsem[key[1]]
                            eng.wait_ge(s, val)
                        ins = r["fn"](eng)
                        if r["dma"] is not None:
                            ins.then_inc(dsem[r["dma"] % NS], 16)
                        elif i in sig[e]:
                            ins.then_inc(esem[e], 1)
                    if e == "sp":
                        for j in range(min(NS, ndma)):
                            cnt = (ndma - 1 - j) // NS + 1
                            if waited.get(("d", j), 0) < 16 * cnt:
                                eng.wait_ge(dsem[j], 16 * cnt)
                return f

            block.tensor(body("pe"))
            block.scalar(body("act"))
            block.vector(body("dve"))
            block.gpsimd(body("pool"))
            block.sync(body("sp"))


class Ring:
    def __init__(self, aps):
        self.items = [(a, Buf()) for a in aps]
        self.i = 0

    def next(self):
        it = self.items[self.i % len(self.items)]
        self.i += 1
        return it


class Arena:
    def __init__(self, t, lo, hi):
        self.t, self.lo, self.hi, self.cur = t, lo, hi, lo

    def sub(self, lo, hi):
        return Arena(self.t, lo, hi)

    def get(self, shape, dt):
        esz = 2 if dt == BF16 else 4
        n = 1
        for s in shape[1:]:
            n *= s
        nb = (n * esz + 31) // 32 * 32
        off = self.cur
        assert off + nb <= self.hi, ("arena overflow", shape, off, nb, self.hi)
        self.cur = off + nb
        ap = self.t[:, off // 4: off // 4 + (n * esz + 3) // 4]
        if dt != F32:
            ap = ap.bitcast(dt)
        if len(shape) == 3:
            ap = ap.rearrange("p (a b) -> p a b", b=shape[2])
        elif len(shape) == 4:
            ap = ap.rearrange("p (a b c) -> p a b c", b=shape[2], c=shape[3])
        if shape[0] < 128:
            ap = ap[0:shape[0]]
        return ap


def KB(x):
    return int(x * 1024)


def build_program(stop_after=None):
    nc = bass.Bass("TRN2", target_bir_lowering=False)

    def din(name, shape):
        return nc.dram_tensor(name, shape, F32, kind="ExternalInput").ap()

    def dout(name, shape):
        return nc.dram_tensor(name, shape, F32, kind="ExternalOutput").ap()

    xp = din("xp", [SEQ, D])
    xs = din("xs", [NSMP, D])
    caches = [din("c1", [NSMP, 128, 1024]), din("c4", [NSMP, 512, 1024]), din("c16", [NSMP, 2048, 1024])]
    st_in = din("st", [NSMP, 4, 128, 256])
    w_in = din("w_in", [D, IN_COLS])
    w_att = din("w_att", [512, D])
    w_ret = din("w_ret", [D, D])
    w_out = din("w_out", [D, D])
    w_up = din("w_up", [D, 4 * D])
    w_down = din("w_down", [4 * D, D])
    k_ident = din("k_ident", [128, 128])
    k_maskb = din("k_maskb", [128, 256])
    k_cos = din("k_cos", [128, 17 * 64])
    k_sin = din("k_sin", [128, 17 * 64])
    k_dt = din("k_dt", [128, 512])
    k_small = din("k_small", [128, 8])
    k_gpre = din("k_gpre", [128, 16])
    k_grep = din("k_grep", [128, 3 * D])
    k_sel = din("k_sel", [16, 16 * 128])
    k_i16s = din("k_i16s", [16, 16])
    k_i16bc = din("k_i16bc", [128, 256])

    yp = dout("yp", [SEQ, D])
    ys = dout("ys", [NSMP, D])
    kvp = [dout("kv1p", [128, 1024]), dout("kv4p", [512, 1024]), dout("kv16p", [2048, 1024])]
    stp = dout("stp", [4, 128, 256])
    kvs = [dout("kv1s", [NSMP, 1024]), dout("kv4s", [NSMP, 1024]), dout("kv16s", [NSMP, 1024])]
    sts = dout("sts", [NSMP, 4, 128, 256])
    x1d = nc.dram_tensor("x1_scratch", [NTOK, D], F32).ap()

    P = Prog(nc)
    TOTAL = 212736
    with ExitStack() as st:
        ar_t = st.enter_context(nc.sbuf_tensor("arena", [128, TOTAL // 4], F32))
        ps_t = [st.enter_context(nc.psum_tensor("ps%d" % i, [128, 1024], F32)) for i in range(4)]
        bankB = [Buf() for _ in range(8)]

        def bank_ap(i):
            return ps_t[i // 2][:, (i % 2) * 512:(i % 2) * 512 + 512]

        class PsumPools:
            def set(self, singles, doubles):
                self.s = Ring([bank_ap(i) for i in singles])
                self.s.items = [(bank_ap(i), bankB[i]) for i in singles]
                self.d_items = [(ps_t[j][:, :], [bankB[2 * j], bankB[2 * j + 1]]) for j in doubles]
                self.di = 0

            def one(self):
                return self.s.next()

            def two(self):
                it = self.d_items[self.di % len(self.d_items)]
                self.di += 1
                return it

        PS = PsumPools()

        A = Arena(ar_t, 0, TOTAL)
        cA = A.sub(0, KB(3))
        ident_f = cA.get([128, 128], F32)
        ident_b = cA.get([128, 128], BF16)
        ones_b = cA.get([128, 128], BF16)
        ones_f = cA.get([128, 128], F32)
        gpre = cA.get([128, 16], F32)
        junk_lo = KB(3)
        X0 = KB(3)
        X1 = X0 + KB(32.5)
        X2 = X1 + KB(16.5)
        X3 = X2 + KB(32.5)
        X4 = X3 + KB(32.5)
        hT = A.sub(X0, X1).get([128, 8, NTOK], BF16)
        attT = A.sub(X1, X2).get([128, 4, NTOK], BF16)
        rgT = A.sub(X2, X3).get([128, 8, NTOK], BF16)
        mT = A.sub(X3, X4).get([128, 8, NTOK], BF16)
        h2T = hT

        def mm(out, lhsT, rhs, start, stop, reads=(), adds=()):
            P.op("pe", lambda e: e.matmul(out, lhsT=lhsT, rhs=rhs, start=start, stop=stop), reads=reads, adds=adds)

        def tr(out, in_, ident, reads=(), adds=()):
            P.op("pe", lambda e: e.transpose(out=out, in_=in_, identity=ident), reads=reads, adds=adds)

        def act(out, in_, func, reads=(), writes=(), adds=(), **kw):
            P.op("act", lambda e: e.activation(out=out, in_=in_, func=func, **kw), reads=reads, writes=writes, adds=adds)

        def tt(eng, out, in0, in1, op, reads=(), writes=(), adds=()):
            P.op(eng, lambda e: e.tensor_tensor(out=out, in0=in0, in1=in1, op=op), reads=reads, writes=writes, adds=adds)

        def ts(eng, out, in0, s1, s2, op0, op1=None, reads=(), writes=(), adds=()):
            if op1 is None:
                P.op(eng, lambda e: e.tensor_scalar(out=out, in0=in0, scalar1=s1, scalar2=None, op0=op0),
                     reads=reads, writes=writes, adds=adds)
            else:
                P.op(eng, lambda e: e.tensor_scalar(out=out, in0=in0, scalar1=s1, scalar2=s2, op0=op0, op1=op1),
                     reads=reads, writes=writes, adds=adds)

        def stt(out, in0, scalar, in1, op0, op1, reads=(), writes=(), adds=()):
            P.op("dve", lambda e: e.scalar_tensor_tensor(out=out, in0=in0, scalar=scalar, in1=in1, op0=op0, op1=op1),
                 reads=reads, writes=writes, adds=adds)

        def cp(eng, out, in_, reads=(), writes=(), adds=()):
            if eng == "act":
                act(out, in_, AF.Copy, reads=reads, writes=writes, adds=adds)
            else:
                P.op(eng, lambda e: e.tensor_copy(out=out, in_=in_), reads=reads, writes=writes, adds=adds)

        def recip(out, in_, reads=(), writes=()):
            P.op("dve", lambda e: e.reciprocal(out=out, in_=in_), reads=reads, writes=writes)

        def wload(dst, src2d, writes):
            P.dma("pool", dst, src2d.rearrange("(kc p) n -> p kc n", p=128), writes=writes)

        def rms_rstd(src, np_, statring, junk, reads):
            stt_, Bs = statring.next()
            act(junk[:np_], src, AF.Square, reads=reads, writes=[Bs], accum_out=stt_[:np_, 0:1])
            act(stt_[:np_, 1:2], stt_[:np_, 0:1], AF.Sqrt, reads=[Bs], writes=[Bs], scale=1.0 / D, bias=EPS)
            recip(stt_[:np_, 2:3], stt_[:np_, 1:2], reads=[Bs], writes=[Bs])
            return stt_[:np_, 2:3], Bs

        def norm_p1(src, np_, R, reads):
            rstd, Bs = rms_rstd(src, np_, R["stat"], R["junk"], reads)
            hb, Bhb = R["hb"].next()
            ts("dve", hb[:np_], src, rstd, None, ALU.mult, reads=list(reads) + [Bs], writes=[Bhb])
            return hb, Bhb

        def norm_p2(hb, Bhb, np_, tok0, gcol0, dstT):
            pT, BpT = PS.one()
            pTv = pT.bitcast(BF16).rearrange("p (a b) -> p a b", b=128)
            for kc in range(8):
                tr(pTv[:, kc, :np_], hb[:np_, kc * 128:(kc + 1) * 128], ident_b[:np_, :np_], reads=[Bhb], adds=[BpT])
            tt("dve", dstT[:, :, tok0:tok0 + np_], pTv[:, :, :np_],
               gpre[:, gcol0:gcol0 + 8].unsqueeze(2).to_broadcast([128, 8, np_]), ALU.mult, reads=[BpT])

        PS.set(list(range(8)), [0, 1, 2, 3])
        Bc = Buf()
        P.dma("sp", ident_f, k_ident, adds=[Bc])
        P.dma("sp", gpre, k_gpre, adds=[Bc])
        cp("dve", ident_b, ident_f, reads=[Bc], writes=[Bc])
        P.op("dve", lambda e: e.memset(ones_b, 1.0), adds=[Bc])
        P.op("dve", lambda e: e.memset(ones_f, 1.0), adds=[Bc])
        P.barrier()

        def phaseA(pre_hook):
            T = A.sub(TOTAL - KB(24), TOTAL)
            R = {"stat": Ring([T.get([128, 4], F32) for _ in range(4)]),
                 "junk": T.get([128, 1024], F32),
                 "hb": Ring([T.get([128, 1024], BF16) for _ in range(3)])}
            xr = Ring([T.get([128, 1024], F32) for _ in range(2)])
            pend = None
            for t in range(17):
                np_ = 128 if t < 16 else 16
                src = xp[t * 128:(t + 1) * 128, :] if t < 16 else xs
                xt, Bx = xr.next()
                P.dma("sp", xt[:np_], src, writes=[Bx])
                hb, Bhb = norm_p1(xt[:np_], np_, R, [Bx])
                if pend is not None:
                    norm_p2(*pend)
                pend = (hb, Bhb, np_, t * 128, 0, hT)
            norm_p2(*pend)
            pre_hook()
            P.barrier()

        qTs = kTs = vTs = None

        def phaseB1():
            T = A.sub(X2, TOTAL)
            vtm = [T.get([128, 16, 512], BF16) for _ in range(3)]
            Bv = [[Buf() for _ in range(16)] for _ in range(3)]
            wr = Ring([T.get([128, 8, 512], BF16) for _ in range(3)])
            kvr = Ring([T.get([128, 2, 512], F32) for _ in range(3)])
            kvsr = Ring([T.get([16, 2, 512], F32) for _ in range(3)])
            acc = T.get([128, 2, SEQ], F32)
            Bacc = [Buf() for _ in range(4)]
            qr_ = Ring([T.get([128, SEQ], BF16) for _ in range(2)])
            kr_ = Ring([T.get([128, SEQ], BF16) for _ in range(2)])
            cr = Ring([T.get([128, 8, 128], BF16) for _ in range(6)])
            ptr = Ring([T.get([128, 256], BF16) for _ in range(4)])
            maskf = T.get([128, 256], F32)
            maskb = T.get([128, 256], BF16)
            sm = T.get([128, 3, 12, 16], F32)
            Bsm = Buf()
            Bm = Buf()
            pre_w = []
            for off_ in (OFF_KA, OFF_VA):
                W_, BW_ = wr.next()
                wload(W_, w_in[:, off_:off_ + 512], [BW_])
                pre_w.append((W_, BW_))
            yield None
            P.dma("sp", maskf, k_maskb, writes=[Bm])
            cp("dve", maskb, maskf, reads=[Bm], writes=[Bm])

            for g in range(3):
                d = DILS[g]
                n = SEQ // d
                nb = n // 128
                if g == 0:
                    (Wk, BWk), (Wv, BWv) = pre_w
                else:
                    Wk, BWk = wr.next()
                    wload(Wk, w_in[:, OFF_KA + g * 512: OFF_KA + (g + 1) * 512], [BWk])
                    Wv, BWv = wr.next()
                    wload(Wv, w_in[:, OFF_VA + g * 512: OFF_VA + (g + 1) * 512], [BWv])
                for c in range(d):
                    for b in range(nb):
                        tile = c * nb + b
                        start = b * 128 * d + c
                        tok = slice(start, start + 128 * d, d)
                        need_k = True if g == 2 else (b == nb - 1)
                        pv, Bpv = PS.one()
                        for kc in range(8):
                            mm(pv, hT[:, kc, tok], Wv[:, kc, :], kc == 0, kc == 7, reads=[BWv], adds=[Bpv])
                        cp("act", vtm[g][:, tile, :], pv, reads=[Bpv], writes=[Bv[g][tile]])
                        if need_k:
                            pk, Bpk = PS.one()
                            for kc in range(8):
                                mm(pk, hT[:, kc, tok], Wk[:, kc, :], kc == 0, kc == 7, reads=[BWk], adds=[Bpk])
                            kv, Bkv = kvr.next()
                            cp("dve", kv[:, 0, :], pk, reads=[Bpk], writes=[Bkv])
                            cp("dve", kv[:, 1, :], pv, reads=[Bpv, Bv[g][tile]], adds=[Bkv])
                            if g == 0:
                                dst = kvp[0]
                            else:
                                dst = kvp[g].rearrange("(i c) n -> c i n", c=d)[c]
                            P.dma("sp", dst, kv.rearrange("p a n -> p (a n)"), reads=[Bkv])
                pk, Bpk = PS.one()
                pv, Bpv = PS.one()
                for kc in range(8):
                    mm(pk[:16, :], hT[:, kc, SEQ:NTOK], Wk[:, kc, :], kc == 0, kc == 7, reads=[BWk], adds=[Bpk])
                for kc in range(8):
                    mm(pv[:16, :], hT[:, kc, SEQ:NTOK], Wv[:, kc, :], kc == 0, kc == 7, reads=[BWv], adds=[Bpv])
                kv, Bkv = kvsr.next()
                cp("dve", kv[:, 0, :], pk[:16, :], reads=[Bpk], writes=[Bkv])
                cp("dve", kv[:, 1, :], pv[:16, :], reads=[Bpv], adds=[Bkv])
                P.dma("sp", kvs[g], kv.rearrange("p a n -> p (a n)"), reads=[Bkv])

            for h in range(4):
                for g in range(3):
                    d = DILS[g]
                    n = SEQ // d
                    nb = n // 128
                    col = (g * 4 + h) * 128
                    Wq, BWq = cr.next()
                    wload(Wq, w_in[:, OFF_QA + col: OFF_QA + col + 128], [BWq])
                    Wk, BWk = cr.next()
                    wload(Wk, w_in[:, OFF_KA + col: OFF_KA + col + 128], [BWk])
                    Wv, BWv = cr.next()
                    wload(Wv, w_in[:, OFF_VA + col: OFF_VA + col + 128], [BWv])
                    qT, BqT = qr_.next()
                    kT, BkT = kr_.next()
                    for (W, BW, dst, Bdst, eng) in ((Wq, BWq, qT, BqT, "act"), (Wk, BWk, kT, BkT, "dve")):
                        for blk in range(4):
                            pq, Bpq = PS.one()
                            for kc in range(8):
                                mm(pq, W[:, kc, :], hT[:, kc, blk * 512:(blk + 1) * 512], kc == 0, kc == 7,
                                   reads=[BW], adds=[Bpq])
                            if d == 1:
                                o_ap, i_ap = dst[:, blk * 512:(blk + 1) * 512], pq
                            else:
                                w_ = 512 // d
                                o_ap = dst.rearrange("p (c i) -> p i c", c=d)[:, blk * w_:(blk + 1) * w_, :]
                                i_ap = pq.rearrange("p (i c) -> p i c", c=d)
                            cp(("act" if (eng == "dve" and blk % 2 == 1) else eng), o_ap, i_ap, reads=[Bpq], adds=[Bdst])
                    for j, (W, BW) in enumerate(((Wq, BWq), (Wk, BWk), (Wv, BWv))):
                        pq, Bpq = PS.one()
                        for kc in range(8):
                            mm(pq[:, 0:16], W[:, kc, :], hT[:, kc, SEQ:NTOK], kc == 0, kc == 7, reads=[BW], adds=[Bpq])
                        cp("act", sm[:, j, g * 4 + h, :], pq[:, 0:16], reads=[Bpq], adds=[Bsm])
                    tiles_ = [(c, b) for c in range(d) for b in range(nb)]

                    def att_s1(c, b, kT=kT, qT=qT, BkT=BkT, BqT=BqT, n=n):
                        pos0 = c * n + b * 128
                        lo = 0 if b > 0 else 128
                        sp_, Bsp = PS.one()
                        mm(sp_[:, 128:256], kT[:, pos0:pos0 + 128], qT[:, pos0:pos0 + 128], True, False,
                           reads=[BkT, BqT], adds=[Bsp])
                        if b > 0:
                            mm(sp_[:, 0:128], kT[:, pos0 - 128:pos0], qT[:, pos0:pos0 + 128], False, False,
                               reads=[BkT, BqT], adds=[Bsp])
                        mm(sp_[:, lo:256], ident_b, maskb[:, lo:256], False, True, reads=[Bm], adds=[Bsp])
                        pt, Bpt = ptr.next()
                        act(pt[:, lo:256], sp_[:, lo:256], AF.Exp, reads=[Bsp], writes=[Bpt], scale=SC_A)
                        return pt, Bpt

                    def att_s2(c, b, pt, Bpt, g=g, h=h, nb=nb):
                        tile = c * nb + b
                        ud, Bud = PS.one()
                        udv = ud[:, 0:256].rearrange("p (a q) -> p a q", a=2)
                        hs = slice(h * 128, (h + 1) * 128)
                        mm(udv[:, 0, :], vtm[g][:, tile, hs], pt[:, 128:256], True, False,
                           reads=[Bv[g][tile], Bpt], adds=[Bud])
                        if b > 0:
                            mm(udv[:, 0, :], vtm[g][:, tile - 1, hs], pt[:, 0:128], False, False,
                               reads=[Bv[g][tile - 1], Bpt], adds=[Bud])
                        mm(udv[:, 1, :], ones_b, pt[:, 128:256], False, b == 0, reads=[Bpt], adds=[Bud])
                        if b > 0:
                            mm(udv[:, 1, :], ones_b, pt[:, 0:128], False, True, reads=[Bpt], adds=[Bud])
                        if g == 0:
                            cp("dve", acc[:, :, b * 128:(b + 1) * 128], udv, reads=[Bud], adds=[Bacc[b // 4]])
                        elif g == 1:
                            av = acc[:, :, b * 512 + c:(b + 1) * 512:4]
                            tt("dve", av, av, udv, ALU.add, reads=[Bud, Bacc[b]], writes=[Bacc[b]])
                        else:
                            av = acc[:, :, c:SEQ:16]
                            tt("dve", av, av, udv, ALU.add, reads=[Bud] + Bacc, writes=Bacc)

                    LOOK = 2
                    pend_ = []
                    for i_, (c, b) in enumerate(tiles_):
                        pend_.append((c, b) + att_s1(c, b))
                        if len(pend_) > LOOK:
                            att_s2(*pend_.pop(0))
                    while pend_:
                        att_s2(*pend_.pop(0))
                recip(acc[:, 1, :], acc[:, 1, :], reads=Bacc, writes=Bacc)
                tt("dve", attT[:, h, 0:SEQ], acc[:, 0, :], acc[:, 1, :], ALU.mult, reads=Bacc)
            P.barrier()
            yield sm

        def phaseB1c(sm):
            PS.set([2, 3, 4, 5, 6, 7], [0])
            T = A.sub(X2, X2 + KB(135))
            qtm = [T.get([16, 512], BF16) for _ in range(3)]
            Bq = [Buf() for _ in range(3)]
            self_f = T.get([16, 16, 128], F32)
            sel = T.get([16, 16, 128], BF16)
            Bsel = Buf()
            P.dma("sp", self_f, k_sel.rearrange("k (s m) -> k s m", m=128), writes=[Bsel])
            cp("dve", sel, self_f, reads=[Bsel], writes=[Bsel])
            kvg_t = [T.get([128, 16, 1024], BF16) for _ in range(3)]
            kvg_B = [[Buf() for _ in range(NSMP)] for _ in range(3)]
            prr = Ring([T.get([128, 512], F32) for _ in range(3)])
            Sc = T.get([128, 3, 16, 4], F32)
            E = T.get([128, 3, 16, 4], BF16)
            BSc = [Buf() for _ in range(3)]
            BE = [Buf() for _ in range(3)]
            e0 = T.get([128, 12, 16], F32)
            t0 = T.get([128, 12, 16], F32)
            numt = T.get([128, 4, 16], F32)
            dent = T.get([128, 4, 16], F32)
            Bt = Buf()
            qTs_, kTs_, vTs_ = sm[:, 0], sm[:, 1], sm[:, 2]
            for g in range(3):
                pq, Bpq = PS.one()
                for h in range(4):
                    tr(pq[:16, h * 128:(h + 1) * 128], qTs_[:, g * 4 + h, :], ident_f, adds=[Bpq])
                cp("act", qtm[g], pq[:16, :], reads=[Bpq], writes=[Bq[g]])
            pnum, Bnum = bank_ap(0), bankB[0]
            pden, Bden = bank_ap(1), bankB[1]
            state = {"first": True}

            def s1(g):
                d = DILS[g]
                L = 128 * d
                kvt, Bkv = kvg_t[g], kvg_B[g]
                for s in range(NSMP):
                    P.dma("pool", kvt[:, s, :], caches[g][s, 0:L:d, :], writes=[Bkv[s]])
                for s in range(NSMP):
                    pb, Bpb = PS.one()
                    mm(pb, sel[:, s, :], qtm[g], True, True, reads=[Bsel, Bq[g]], adds=[Bpb])
                    pr, Bpr = prr.next()
                    tt("dve", pr, kvt[:, s, 0:512], pb, ALU.mult, reads=[Bkv[s], Bpb], writes=[Bpr])
                    P.op("dve", lambda e, o=Sc[:, g, s, :], i=pr.rearrange("p (h x) -> p h x", h=4):
                         e.tensor_reduce(out=o, in_=i, axis=AX.X, op=ALU.add), reads=[Bpr], adds=[BSc[g]])
                act(E[:, g].rearrange("p s h -> p (s h)"), Sc[:, g].rearrange("p s h -> p (s h)"), AF.Exp,
                    reads=[BSc[g]], writes=[BE[g]], scale=SC_A)
                return kvt, Bkv

            def s2(g, kvt, Bkv):
                for s in range(NSMP):
                    for h in range(4):
                        mm(pnum[:, h * 16 + s:h * 16 + s + 1], kvt[:, s, 512 + h * 128:512 + (h + 1) * 128],
                           E[:, g, s, h:h + 1], state["first"], False, reads=[Bkv[s], BE[g]], adds=[Bnum])
                        state["first"] = False
                mm(pden[:, 0:64], ones_b, E[:, g].rearrange("p s h -> p (s h)"), g == 0, g == 2,
                   reads=[BE[g]], adds=[Bden])

            k0 = s1(0)
            k1 = s1(1)
            s2(0, *k0)
            k2 = s1(2)
            s2(1, *k1)
            s2(2, *k2)
            tt("dve", t0, qTs_, kTs_, ALU.mult, writes=[Bt])
            ps0, Bps0 = PS.one()
            mm(ps0[:, 0:192], ones_f, t0.rearrange("p a s -> p (a s)"), True, True, reads=[Bt], adds=[Bps0])
            act(e0, ps0[:, 0:192].rearrange("p (a s) -> p a s", s=16), AF.Exp, reads=[Bps0], writes=[Bt], scale=SC_A)
            tt("dve", t0, e0, vTs_, ALU.mult, reads=[Bt], writes=[Bt])
            tt("dve", numt, pnum[:, 0:64].rearrange("p (h s) -> p h s", s=16), t0[:, 0:4, :], ALU.add,
               reads=[Bnum, Bt], writes=[Bt])
            tt("dve", dent, pden[:, 0:64].rearrange("p (s h) -> p h s", h=4), e0[:, 0:4, :], ALU.add,
               reads=[Bden, Bt], writes=[Bt])
            for g in (1, 2):
                tt("dve", numt, numt, t0[:, g * 4:(g + 1) * 4, :], ALU.add, reads=[Bt], writes=[Bt])
                tt("dve", dent, dent, e0[:, g * 4:(g + 1) * 4, :], ALU.add, reads=[Bt], writes=[Bt])
            recip(dent, dent, reads=[Bt], writes=[Bt])
            tt("dve", attT[:, :, SEQ:NTOK], numt, dent, ALU.mult, reads=[Bt])
            P.barrier()

        def phaseB2():
            PS.set([0, 1, 2, 3], [2, 3])
            T = A.sub(X3, TOTAL)
            Wq = T.get([128, 8, 512], BF16)
            Wk = T.get([128, 8, 512], BF16)
            Wv = T.get([128, 8, 1024], BF16)
            Wg = T.get([128, 8, 1024], BF16)
            BW = Buf()
            P.dma("pool", Wq, w_in[:, OFF_QR:OFF_QR + 512].rearrange("(kc p) n -> p kc n", p=128), adds=[BW])
            P.dma("pool", Wk, w_in[:, OFF_KR:OFF_KR + 512].rearrange("(kc p) n -> p kc n", p=128), adds=[BW])
            for hf in range(2):
                P.dma("pool", Wv[:, :, hf * 512:(hf + 1) * 512],
                      w_in[:, OFF_VR + hf * 512:OFF_VR + (hf + 1) * 512].rearrange("(kc p) n -> p kc n", p=128), adds=[BW])
                P.dma("pool", Wg[:, :, hf * 512:(hf + 1) * 512],
                      w_in[:, OFF_GR + hf * 512:OFF_GR + (hf + 1) * 512].rearrange("(kc p) n -> p kc n", p=128), adds=[BW])
            cos = T.get([128, 17, 64], F32)
            sin = T.get([128, 17, 64], F32)
            dtb = T.get([128, 4, 128], F32)
            small = T.get([128, 8], F32)
            gret = T.get([128, 1024], F32)
            i16s = T.get([16, 16], F32)
            i16bc = T.get([128, 16, 16], F32)
            Bk = Buf()
            P.dma("sp", cos, k_cos.rearrange("p (t j) -> p t j", j=64), adds=[Bk])
            P.dma("sp", sin, k_sin.rearrange("p (t j) -> p t j", j=64), adds=[Bk])
            P.dma("sp", dtb, k_dt.rearrange("p (h t) -> p h t", t=128), adds=[Bk])
            P.dma("sp", small, k_small, adds=[Bk])
            P.dma("sp", gret, k_grep[:, 2 * D:3 * D], adds=[Bk])
            P.dma("sp", i16s, k_i16s, adds=[Bk])
            P.dma("sp", i16bc, k_i16bc.rearrange("p (s j) -> p s j", j=16), adds=[Bk])
            S = T.get([128, 4, 256], F32)
            Sbf = T.get([128, 4, 256], BF16)
            BS = [Buf() for _ in range(4)]
            BSb = [Buf() for _ in range(4)]
            tmpr = Ring([T.get([128, 4, 64], F32) for _ in range(4)])
            sgr = Ring([T.get([128, 1024], F32) for _ in range(2)])
            rnr = Ring([T.get([128, 1024], F32) for _ in range(1)])
            rgr = Ring([T.get([128, 1024], BF16) for _ in range(2)])
            str_ = Ring([T.get([128, 4, 8], F32) for _ in range(2)])
            rsr = Ring([T.get([128, 8], F32) for _ in range(2)])
            LA = T.cur

            def rope(psrc, Bp, np_, t, dst, Bdst):
                xv = psrc.rearrange("p (h a j) -> p h a j", h=4, a=2)
                dv = dst.rearrange("p (h a j) -> p h a j", h=4, a=2)
                cb = cos[:np_, t, :].unsqueeze(1).to_broadcast([np_, 4, 64])
                sb = sin[:np_, t, :].unsqueeze(1).to_broadcast([np_, 4, 64])
                ta, Ba = tmpr.next()
                tb, Bb = tmpr.next()
                tt("dve", ta[:np_], xv[:, :, 0, :], cb, ALU.mult, reads=[Bp, Bk], writes=[Ba])
                tt("dve", tb[:np_], xv[:, :, 1, :], sb, ALU.mult, reads=[Bp, Bk], writes=[Bb])
                tt("pool", dv[:, :, 0, :], ta[:np_], tb[:np_], ALU.subtract, reads=[Ba, Bb], adds=[Bdst])
                tc_, Bc_ = tmpr.next()
                td, Bd = tmpr.next()
                tt("dve", tc_[:np_], xv[:, :, 0, :], sb, ALU.mult, reads=[Bp, Bk], writes=[Bc_])
                tt("dve", td[:np_], xv[:, :, 1, :], cb, ALU.mult, reads=[Bp, Bk], writes=[Bd])
                tt("pool", dv[:, :, 1, :], tc_[:np_], td[:np_], ALU.add, reads=[Bc_, Bd], adds=[Bdst])

            def groupnorm_gate_T(pO, BpO, sg, Bsg, np_, tok0, defer=False):
                st6, Bst = str_.next()
                for h in range(4):
                    P.op("dve", lambda e, o=st6[:np_, h, 0:6], i=pO[:, h * 256:(h + 1) * 256]: e.bn_stats(out=o, in_=i),
                         reads=BpO, adds=[Bst])
                for h in range(4):
                    P.op("dve", lambda e, o=st6[:np_, h, 6:8], i=st6[:np_, h, 0:6]: e.bn_aggr(out=o, in_=i),
                         reads=[Bst], writes=[Bst])
                rs, Brs = rsr.next()
                act(rs[:np_, 0:4], st6[:np_, :, 7], AF.Sqrt, reads=[Bst], writes=[Brs], bias=EPS)
                recip(rs[:np_, 4:8], rs[:np_, 0:4], reads=[Brs], writes=[Brs])
                rn, Brn = rnr.next()
                for h in range(4):
                    ts("dve", rn[:np_, h * 256:(h + 1) * 256], pO[:, h * 256:(h + 1) * 256], st6[:np_, h, 6:7],
                       rs[:np_, 4 + h:5 + h], ALU.subtract, ALU.mult, reads=list(BpO) + [Bst, Brs],
                       writes=([Brn] if h == 0 else ()), adds=(() if h == 0 else [Brn]))
                tt("dve", rn[:np_], rn[:np_], gret[:np_], ALU.mult, reads=[Brn, Bk], writes=[Brn])
                rg, Brg = rgr.next()
                tt("pool", rg[:np_], rn[:np_], sg[:np_], ALU.mult, reads=[Brn, Bsg], writes=[Brg])

                def fin():
                    pT, BpT = PS.one()
                    pTv = pT.bitcast(BF16).rearrange("p (a b) -> p a b", b=128)
                    for kc in range(8):
                        tr(pTv[:, kc, :np_], rg[:np_, kc * 128:(kc + 1) * 128], ident_b[:np_, :np_], reads=[Brg], adds=[BpT])
                    cp("act", rgT[:, :, tok0:tok0 + np_], pTv[:, :, :np_], reads=[BpT])
                if defer:
                    return fin
                fin()

            def proj_tm(W, ncol, np_, tok):
                if ncol == 512:
                    p_, Bp_ = PS.one()
                    for kc in range(8):
                        mm(p_[:np_], hT[:, kc, tok], W[:, kc, :], kc == 0, kc == 7, reads=[BW], adds=[Bp_])
                    return p_[:np_], [Bp_]
                p_, Bp_ = PS.two()
                for hf in range(2):
                    for kc in range(8):
                        mm(p_[:np_, hf * 512:(hf + 1) * 512], hT[:, kc, tok], W[:, kc, hf * 512:(hf + 1) * 512],
                           kc == 0, kc == 7, reads=[BW], adds=[Bp_[hf]])
                return p_[:np_], Bp_

            L1 = Arena(ar_t, LA, TOTAL)
            qrr = Ring([L1.get([128, 512], BF16) for _ in range(2)])
            krr = Ring([L1.get([128, 512], BF16) for _ in range(2)])
            qpr = Ring([L1.get([128, 512], BF16) for _ in range(2)])
            kpr = Ring([L1.get([128, 512], BF16) for _ in range(2)])
            qTr = Ring([L1.get([128, 8, 128], BF16) for _ in range(2)])
            kTr = Ring([L1.get([128, 4, 128], BF16) for _ in range(2)])
            vrr = Ring([L1.get([128, 1024], BF16) for _ in range(2)])
            stm = Ring([L1.get([128, 128], BF16) for _ in range(5)])
            def b2_s1(t):
                tok = slice(t * 128, (t + 1) * 128)
                pq, Bpq = proj_tm(Wq, 512, 128, tok)
                qr, Bqr = qrr.next()
                rope(pq, Bpq[0], 128, t, qr, Bqr)
                pk, Bpk = proj_tm(Wk, 512, 128, tok)
                kr, Bkr = krr.next()
                rope(pk, Bpk[0], 128, t, kr, Bkr)
                pv, Bpv = proj_tm(Wv, 1024, 128, tok)
                vr, Bvr = vrr.next()
                cp("act", vr, pv, reads=Bpv, writes=[Bvr])
                pg, Bpg = proj_tm(Wg, 1024, 128, tok)
                sg, Bsg = sgr.next()
                act(sg, pg, AF.Silu, reads=Bpg, writes=[Bsg])
                qp, Bqp = qpr.next()
                kp, Bkp = kpr.next()
                tt("pool", qp.rearrange("p (h x) -> p h x", h=4), qr.rearrange("p (h x) -> p h x", h=4),
                   small[:, 0:4].unsqueeze(2).to_broadcast([128, 4, 128]), ALU.mult, reads=[Bqr, Bk], writes=[Bqp])
                tt("pool", kp.rearrange("p (h x) -> p h x", h=4), kr.rearrange("p (h x) -> p h x", h=4),
                   small[:, 4:8].unsqueeze(2).to_broadcast([128, 4, 128]), ALU.mult, reads=[Bkr, Bk], writes=[Bkp])
                return dict(t=t, qr=qr, Bqr=Bqr, kr=kr, Bkr=Bkr, vr=vr, Bvr=Bvr, sg=sg, Bsg=Bsg,
                            qp=qp, Bqp=Bqp, kp=kp, Bkp=Bkp)

            def b2_t(c):
                qr, Bqr, kr, Bkr = c["qr"], c["Bqr"], c["kr"], c["Bkr"]
                qp, Bqp = c["qp"], c["Bqp"]
                pT1, BpT1 = PS.one()
                pT1v = pT1.bitcast(BF16).rearrange("p (a b) -> p a b", b=128)
                for h in range(4):
                    tr(pT1v[:, h, :], qr[:, h * 128:(h + 1) * 128], ident_b, reads=[Bqr], adds=[BpT1])
                for h in range(4):
                    tr(pT1v[:, 4 + h, :], qp[:, h * 128:(h + 1) * 128], ident_b, reads=[Bqp], adds=[BpT1])
                qT, BqT = qTr.next()
                cp("act", qT, pT1v, reads=[BpT1], writes=[BqT])
                pT2, BpT2 = PS.one()
                pT2v = pT2.bitcast(BF16).rearrange("p (a b) -> p a b", b=128)
                for h in range(4):
                    tr(pT2v[:, h, :], kr[:, h * 128:(h + 1) * 128], ident_b, reads=[Bkr], adds=[BpT2])
                kT, BkT = kTr.next()
                cp("dve", kT, pT2v[:, 0:4, :], reads=[BpT2], writes=[BkT])
                c["qT"], c["BqT"], c["kT"], c["BkT"] = qT, BqT, kT, BkT

            def b2_s2(c):
                t = c["t"]
                vr, Bvr, kp, Bkp = c["vr"], c["Bvr"], c["kp"], c["Bkp"]
                qT, BqT, kT, BkT = c["qT"], c["BqT"], c["kT"], c["BkT"]
                pO, BpO = PS.two()
                pSU, BpSU = PS.two()
                sTs = []
                for h in range(4):
                    psT, BpsT = PS.one()
                    mm(psT[:, 0:128], kT[:, h, :], qT[:, h, :], True, True, reads=[BkT, BqT], adds=[BpsT])
                    sT, BsT = stm.next()
                    tt("dve", sT, psT[:, 0:128], dtb[:, h, :], ALU.mult, reads=[BpsT, Bk], writes=[BsT])
                    sTs.append((sT, BsT))
                for h in range(4):
                    oh = pO[:, h * 256:(h + 1) * 256]
                    if t > 0:
                        mm(oh, qT[:, 4 + h, :], Sbf[:, h, :], h % 2 == 0, False, reads=[BqT, BSb[h]], adds=[BpO[h // 2]])
                for h in range(4):
                    sT, BsT = sTs[h]
                    oh = pO[:, h * 256:(h + 1) * 256]
                    mm(oh, sT, vr[:, h * 256:(h + 1) * 256], (t == 0 and h % 2 == 0), True, reads=[BsT, Bvr],
                       adds=[BpO[h // 2]])
                for h in range(4):
                    su = pSU[:, h * 256:(h + 1) * 256]
                    mm(su, kp[:, h * 128:(h + 1) * 128], vr[:, h * 256:(h + 1) * 256], h % 2 == 0, True,
                       reads=[Bkp, Bvr], adds=[BpSU[h // 2]])
                for h in range(4):
                    su = pSU[:, h * 256:(h + 1) * 256]
                    if t == 0:
                        cp("dve", S[:, h, :], su, reads=[BpSU[h // 2]], writes=[BS[h]])
                    else:
                        stt(S[:, h, :], S[:, h, :], GAM[h] ** 128, su, ALU.mult, ALU.add,
                            reads=[BpSU[h // 2], BS[h]], writes=[BS[h]])
                    if t < 15:
                        cp("act", Sbf[:, h, :], S[:, h, :], reads=[BS[h]], writes=[BSb[h]])
                c["pO"], c["BpO"] = pO, BpO

            ctx = {}
            ctx[0] = b2_s1(0)
            prev3 = None
            for t in range(16):
                b2_t(ctx[t])
                if t + 1 < 16:
                    ctx[t + 1] = b2_s1(t + 1)
                b2_s2(ctx[t])
                c = ctx[t]
                g3 = groupnorm_gate_T(c["pO"], c["BpO"], c["sg"], c["Bsg"], 128, t * 128, defer=True)
                if prev3 is not None:
                    prev3()
                prev3 = g3
            prev3()
            P.dma("sp", stp.rearrange("h p v -> p h v"), S, reads=BS)
            P.barrier()

            L2 = Arena(ar_t, LA, TOTAL)
            qr = L2.get([16, 512], F32)
            kr = L2.get([16, 512], F32)
            vrs = L2.get([16, 1024], BF16)
            qT2 = L2.get([128, 4, 16], F32)
            str2 = Ring([L2.get([128, 4, 256], F32) for _ in range(3)])
            snr = Ring([L2.get([128, 4, 256], F32) for _ in range(2)])
            vsr = Ring([L2.get([16, 512], BF16) for _ in range(2)])
            qsr = Ring([L2.get([128, 4, 16], BF16) for _ in range(2)])
            snbr = Ring([L2.get([128, 4, 256], BF16) for _ in range(2)])
            Bqr, Bkr, Bvrs, BqT2 = Buf(), Buf(), Buf(), Buf()
            tok = slice(SEQ, NTOK)
            pq, Bpq = proj_tm(Wq, 512, 16, tok)
            rope(pq, Bpq[0], 16, 16, qr, Bqr)
            pk, Bpk = proj_tm(Wk, 512, 16, tok)
            rope(pk, Bpk[0], 16, 16, kr, Bkr)
            pv, Bpv = proj_tm(Wv, 1024, 16, tok)
            cp("act", vrs, pv, reads=Bpv, writes=[Bvrs])
            pg, Bpg = proj_tm(Wg, 1024, 16, tok)
            sg, Bsg = sgr.next()
            act(sg[:16], pg, AF.Silu, reads=Bpg, writes=[Bsg])
            pTq, BpTq = PS.one()
            for h in range(4):
                tr(pTq[:, h * 16:(h + 1) * 16], qr[:, h * 128:(h + 1) * 128], ident_f[:16, :16], reads=[Bqr], adds=[BpTq])
            cp("act", qT2, pTq[:, 0:64].rearrange("p (h s) -> p h s", s=16), reads=[BpTq], writes=[BqT2])
            pOs, BpOs = PS.two()
            Sts = {}

            def st_load(s_):
                St_, BSt_ = str2.next()
                P.dma("sp", St_, st_in[s_].rearrange("h p v -> p h v"), writes=[BSt_])
                Sts[s_] = (St_, BSt_)
            for s_ in range(3):
                st_load(s_)
            for s in range(NSMP):
                vs, Bvs = vsr.next()
                ts("dve", vs, kr, i16s[:, s:s + 1], None, ALU.mult, reads=[Bkr, Bk], writes=[Bvs])
                St, BSt = Sts.pop(s)
                Sn, BSn = snr.next()
                for hh in range(2):
                    po, Bpo = PS.one()
                    for h2 in range(2):
                        h = hh * 2 + h2
                        mm(po[:, h2 * 256:(h2 + 1) * 256], vs[:, h * 128:(h + 1) * 128],
                           vrs[:, h * 256:(h + 1) * 256], h2 == 0, h2 == 1, reads=[Bvrs, Bvs], adds=[Bpo])
                    for h2 in range(2):
                        h = hh * 2 + h2
                        stt(Sn[:, h, :], St[:, h, :], GAM[h], po[:, h2 * 256:(h2 + 1) * 256], ALU.mult, ALU.add,
                            reads=[BSt, Bpo], adds=[BSn])
                P.dma("act", sts[s].rearrange("h p v -> p h v"), Sn, reads=[BSn])
                if s + 3 < NSMP:
                    st_load(s + 3)
                qs, Bqs = qsr.next()
                tt("dve", qs, qT2[:, :, s:s + 1].to_broadcast([128, 4, 16]),
                   i16bc[:, s, :].unsqueeze(1).to_broadcast([128, 4, 16]), ALU.mult, reads=[BqT2, Bk], writes=[Bqs])
                Snb, BSnb = snbr.next()
                cp("act", Snb, Sn, reads=[BSn], writes=[BSnb])
                for h in range(4):
                    mm(pOs[:16, h * 256:(h + 1) * 256], qs[:, h, :], Snb[:, h, :], (s == 0 and h % 2 == 0),
                       (s == NSMP - 1), reads=[Bqs, BSnb], adds=[BpOs[h // 2]])
            groupnorm_gate_T(pOs[:16], BpOs, sg, Bsg, 16, SEQ)
            P.barrier()

        def phaseC1():
            PS.set(list(range(8)), [0])
            T = A.sub(X4, TOTAL)
            cr = Ring([T.get([128, 8, 128], BF16) for _ in range(8)])
            sgr_ = Ring([T.get([128, 512], F32) for _ in range(4)])
            m1r = Ring([T.get([128, 512], F32) for _ in range(4)])
            for c in range(8):
                cs = slice(c * 128, (c + 1) * 128)
                Wa, BWa = cr.next()
                wload(Wa[:, 0:4, :], w_att[:, cs], [BWa])
                Wr, BWr = cr.next()
                wload(Wr, w_ret[:, cs], [BWr])
                Wga, BWga = cr.next()
                wload(Wga, w_in[:, OFF_GA + c * 128:OFF_GA + (c + 1) * 128], [BWga])
                Wgm, BWgm = cr.next()
                wload(Wgm, w_in[:, OFF_GM + c * 128:OFF_GM + (c + 1) * 128], [BWgm])
                for (t0_, n) in TB:
                    tk = slice(t0_, t0_ + n)
                    pa, Bpa = PS.one()
                    for kc in range(4):
                        mm(pa[:, :n], Wa[:, kc, :], attT[:, kc, tk], kc == 0, kc == 3, reads=[BWa], adds=[Bpa])
                    pga, Bpga = PS.one()
                    for kc in range(8):
                        mm(pga[:, :n], Wga[:, kc, :], hT[:, kc, tk], kc == 0, kc == 7, reads=[BWga], adds=[Bpga])
                    sga, Bsga = sgr_.next()
                    act(sga[:, :n], pga[:, :n], AF.Sigmoid, reads=[Bpga], writes=[Bsga])
                    m1, Bm1 = m1r.next()
                    tt("dve", m1[:, :n], pa[:, :n], sga[:, :n], ALU.mult, reads=[Bpa, Bsga], writes=[Bm1])
                    pr, Bpr = PS.one()
                    for kc in range(8):
                        mm(pr[:, :n], Wr[:, kc, :], rgT[:, kc, tk], kc == 0, kc == 7, reads=[BWr], adds=[Bpr])
                    pgm, Bpgm = PS.one()
                    for kc in range(8):
                        mm(pgm[:, :n], Wgm[:, kc, :], hT[:, kc, tk], kc == 0, kc == 7, reads=[BWgm], adds=[Bpgm])
                    sgm, Bsgm = sgr_.next()
                    act(sgm[:, :n], pgm[:, :n], AF.Sigmoid, reads=[Bpgm], writes=[Bsgm])
                    m2, Bm2 = m1r.next()
                    tt("dve", m2[:, :n], pr[:, :n], sgm[:, :n], ALU.mult, reads=[Bpr, Bsgm], writes=[Bm2])
                    tt("dve", mT[:, c, tk], m1[:, :n], m2[:, :n], ALU.add, reads=[Bm1, Bm2])
            P.barrier()

        def phaseC2():
            PS.set([0, 1, 2, 3], [2, 3])
            T1 = A.sub(X1, X3)
            T2 = A.sub(X4, TOTAL)
            Wo = T1.get([128, 8, 1024], BF16)
            BWo = Buf()
            for hf in range(2):
                P.dma("pool", Wo[:, :, hf * 512:(hf + 1) * 512],
                      w_out[:, hf * 512:(hf + 1) * 512].rearrange("(kc p) n -> p kc n", p=128), adds=[BWo])
            gpost = T1.get([128, 1024], F32)
            Bg = Buf()
            P.dma("sp", gpost, k_grep[:, 0:D], writes=[Bg])
            R = {"stat": Ring([T1.get([128, 4], F32) for _ in range(6)]),
                 "junk": T1.get([128, 1024], F32),
                 "hb": Ring([T1.get([128, 1024], BF16) for _ in range(3)])}
            xr = Ring([T2.get([128, 1024], F32) for _ in range(4)])
            tmr = Ring([T2.get([128, 1024], F32) for _ in range(2)])
            x1r = Ring([T2.get([128, 1024], F32) for _ in range(4)])
            def c2_s1(t):
                np_ = 128 if t < 16 else 16
                tok = slice(t * 128, t * 128 + np_)
                src = xp[t * 128:(t + 1) * 128, :] if t < 16 else xs
                xt, Bx = xr.next()
                P.dma("sp", xt[:np_], src, writes=[Bx])
                po, Bpo = PS.two()
                for hf in range(2):
                    for kc in range(8):
                        mm(po[:np_, hf * 512:(hf + 1) * 512], mT[:, kc, tok], Wo[:, kc, hf * 512:(hf + 1) * 512],
                           kc == 0, kc == 7, reads=[BWo], adds=[Bpo[hf]])
                return (t, np_, tok, xt, Bx, po, Bpo)

            def c2_s2a(t, np_, tok, xt, Bx, po, Bpo):
                rstd, Bs = rms_rstd(po[:np_], np_, R["stat"], R["junk"], Bpo)
                tm, Btm = tmr.next()
                stt(tm[:np_], po[:np_], rstd, gpost[:np_], ALU.mult, ALU.mult, reads=list(Bpo) + [Bs, Bg], writes=[Btm])
                x1, Bx1 = x1r.next()
                tt("pool", x1[:np_], xt[:np_], tm[:np_], ALU.add, reads=[Bx, Btm], writes=[Bx1])
                P.dma("sp", x1d[tok, :], x1[:np_], reads=[Bx1])
                return (t, np_, x1, Bx1)

            def c2_s2b(t, np_, x1, Bx1):
                hb, Bhb = norm_p1(x1[:np_], np_, R, [Bx1])
                return (hb, Bhb, np_, t * 128, 8, h2T)

            NT_ = 17
            st1, st2a, st2b = {}, {}, {}
            for i in range(NT_ + 3):
                if i < NT_:
                    st1[i] = c2_s1(i)
                if 0 <= i - 1 < NT_:
                    st2a[i - 1] = c2_s2a(*st1.pop(i - 1))
                if 0 <= i - 2 < NT_:
                    st2b[i - 2] = c2_s2b(*st2a.pop(i - 2))
                if 0 <= i - 3 < NT_:
                    norm_p2(*st2b.pop(i - 3))
            P.barrier()

        def phaseD():
            PS.set([0, 1, 2, 3], [2, 3])
            T = A.sub(X1, TOTAL)
            Wd = T.get([128, 32, 1024], BF16)
            BWd = Buf()
            wr = Ring([T.get([128, 8, 512], BF16) for _ in range(3)])
            Wu0, BWu0 = wr.next()
            wload(Wu0, w_up[:, 0:512], [BWu0])
            def wd_piece(i_):
                q4, hf = i_ // 2, i_ % 2
                P.dma("pool", Wd[:, q4 * 8:(q4 + 1) * 8, hf * 512:(hf + 1) * 512],
                      w_down[q4 * 1024:(q4 + 1) * 1024, hf * 512:(hf + 1) * 512].rearrange("(kc p) n -> p kc n", p=128),
                      adds=[BWd])
            gpost = T.get([128, 1024], F32)
            Bg = Buf()
            P.dma("sp", gpost, k_grep[:, D:2 * D], writes=[Bg])
            uT = T.get([128, 32, 768], BF16)
            BuT = [Buf() for _ in range(32)]
            rr = Ring([T.get([128, 512], F32) for _ in range(3)])
            statr = Ring([T.get([128, 4], F32) for _ in range(4)])
            junk = T.get([128, 1024], F32)
            tmr = Ring([T.get([128, 1024], F32) for _ in range(1)])
            x1r = Ring([T.get([128, 1024], F32) for _ in range(2)])
            yr = Ring([T.get([128, 1024], F32) for _ in range(2)])
            groups = [[(0, 512), (512, 256)], [(768, 512), (1280, 256)], [(1536, 512), (2048, 16)]]
            for gi, grp in enumerate(groups):
                for fb in range(8):
                    if gi == 0 and fb == 0:
                        Wu, BWu = Wu0, BWu0
                    else:
                        Wu, BWu = wr.next()
                        wload(Wu, w_up[:, fb * 512:(fb + 1) * 512], [BWu])
                    if gi == 0:
                        wd_piece(fb)
                    for j in range(4):
                        f = fb * 4 + j
                        off = 0
                        for (t0_, n) in grp:
                            pu, Bpu = PS.one()
                            for kc in range(8):
                                mm(pu[:, :n], Wu[:, kc, j * 128:(j + 1) * 128], h2T[:, kc, t0_:t0_ + n], kc == 0, kc == 7,
                                   reads=[BWu], adds=[Bpu])
                            r, Br = rr.next()
                            act(r[:, :n], pu[:, :n], AF.Relu, reads=[Bpu], writes=[Br])
                            tt("dve", uT[:, f, off:off + n], r[:, :n], r[:, :n], ALU.mult, reads=[Br], adds=[BuT[f]])
                            off += n
                tiles = []
                off = 0
                for (t0_, n) in grp:
                    for i in range(0, n, 128):
                        tiles.append((t0_ + i, off + i, min(128, n - i)))
                    off += n
                for (tok0, loc, np_) in tiles:
                    x1, Bx1 = x1r.next()
                    P.dma("sp", x1[:np_], x1d[tok0:tok0 + np_, :], writes=[Bx1])
                    py, Bpy = PS.two()
                    for hf in range(2):
                        for f in range(32):
                            mm(py[:np_, hf * 512:(hf + 1) * 512], uT[:, f, loc:loc + np_], Wd[:, f, hf * 512:(hf + 1) * 512],
                               f == 0, f == 31, reads=[BuT[f], BWd], adds=[Bpy[hf]])
                    rstd, Bs = rms_rstd(py[:np_], np_, statr, junk, Bpy)
                    tm, Btm = tmr.next()
                    stt(tm[:np_], py[:np_], rstd, gpost[:np_], ALU.mult, ALU.mult, reads=list(Bpy) + [Bs, Bg], writes=[Btm])
                    y, By = yr.next()
                    tt("dve", y[:np_], x1[:np_], tm[:np_], ALU.add, reads=[Bx1, Btm], writes=[By])
                    dst = yp[tok0:tok0 + np_, :] if tok0 < SEQ else ys
                    P.dma("sp", dst, y[:np_], reads=[By])

        genB1 = phaseB1()
        phaseA(lambda: next(genB1))
        sm = next(genB1)
        phaseB1c(sm)
        phaseB2()
        phaseC1()
        phaseC2()
        phaseD()
        P.emit()
    return nc


def _constants():
    f32 = np.float32
    c = {}
    c["k_ident"] = np.eye(128, dtype=f32)
    ip = np.arange(128)[:, None]
    iq = np.arange(128)[None, :]
    mprev = np.where(ip >= iq, 0.0, NEG)
    mown = np.where(ip <= iq, 0.0, NEG)
    c["k_maskb"] = np.concatenate([mprev, mown], axis=1).astype(f32)
    half = 64
    inv = (f32(10000.0) ** (-(np.arange(half, dtype=f32) / f32(half)))).astype(f32)
    pos = np.zeros((128, 17), dtype=f32)
    for t in range(16):
        pos[:, t] = t * 128 + np.arange(128)
    pos[:, 16] = PAST
    ang = (pos[:, :, None] * inv[None, None, :]).astype(f32)
    c["k_cos"] = np.cos(ang).astype(f32).reshape(128, 17 * 64)
    c["k_sin"] = np.sin(ang).astype(f32).reshape(128, 17 * 64)
    g = np.array(GAM, dtype=np.float64)
    tq = np.arange(128)
    rel = tq[None, :] - tq[:, None]
    dt = np.zeros((128, 4, 128), dtype=np.float64)
    for h in range(4):
        dt[:, h, :] = np.where(rel >= 0, g[h] ** np.maximum(rel, 0), 0.0) * SC_A
    c["k_dt"] = dt.reshape(128, 512).astype(f32)
    small = np.zeros((128, 8), dtype=np.float64)
    for h in range(4):
        small[:, h] = g[h] ** (tq + 1)
        small[:, 4 + h] = SC_A * g[h] ** (127 - tq)
    c["k_small"] = small.astype(f32)
    sel = np.zeros((16, 16, 128), dtype=f32)
    for s in range(16):
        sel[s, s, :] = 1.0
    c["k_sel"] = sel.reshape(16, 2048)
    c["k_i16s"] = (np.eye(16) * SC_A).astype(f32)
    bc = np.zeros((128, 16, 16), dtype=f32)
    for s in range(16):
        bc[:, s, s] = 1.0
    c["k_i16bc"] = bc.reshape(128, 256)
    return c


_NC_CACHE = {}


def kernel(x_prompt, x_sample, cache_kv_d1, cache_kv_d4, cache_kv_d16, state_ret,
           w_in, w_att_br, w_ret_br, w_out, w_up, w_down,
           g_ret_norm, g_pre_mix, g_post_mix, g_pre_mlp, g_post_mlp):
    f32 = np.float32
    A_ = lambda a: np.ascontiguousarray(np.asarray(a, dtype=f32))
    if "nc" not in _NC_CACHE:
        _NC_CACHE["nc"] = build_program()
    nc = _NC_CACHE["nc"]
    consts = _constants()
    gpre = np.concatenate([A_(g_pre_mix)[0].reshape(8, 128).T, A_(g_pre_mlp)[0].reshape(8, 128).T], axis=1)
    grep = np.concatenate([np.broadcast_to(A_(g_post_mix)[0], (128, D)), np.broadcast_to(A_(g_post_mlp)[0], (128, D)),
                           np.broadcast_to(A_(g_ret_norm)[0], (128, D))], axis=1)
    shared = dict(consts)
    shared.update({
        "w_in": A_(w_in)[0], "w_att": A_(w_att_br)[0], "w_ret": A_(w_ret_br)[0], "w_out": A_(w_out)[0],
        "w_up": A_(w_up)[0], "w_down": A_(w_down)[0],
        "k_gpre": np.ascontiguousarray(gpre, dtype=f32), "k_grep": np.ascontiguousarray(grep, dtype=f32),
    })
    xpr, xsm = A_(x_prompt), A_(x_sample)
    c1, c4, c16, stt_ = A_(cache_kv_d1)[0], A_(cache_kv_d4)[0], A_(cache_kv_d16)[0], A_(state_ret)[0]
    in_maps = []
    for i in range(8):
        sl = slice(i * NSMP, (i + 1) * NSMP)
        m = dict(shared)
        m["xp"] = xpr[i]
        m["xs"] = xsm[sl, 0, :]
        m["c1"] = c1[sl].reshape(NSMP, 128, 1024)
        m["c4"] = c4[sl].reshape(NSMP, 512, 1024)
        m["c16"] = c16[sl].reshape(NSMP, 2048, 1024)
        m["st"] = stt_[sl]
        in_maps.append(m)
    res = run_bass_kernel_spmd(nc, in_maps, core_ids=list(range(8)))
    R = res.results
    cat = lambda name: np.stack([np.asarray(r[name]) for r in R], axis=0)
    y_p = cat("yp")
    y_s = np.concatenate([np.asarray(r["ys"]) for r in R], axis=0).reshape(128, 1, D)
    kv1p = cat("kv1p").reshape(1, 8, 128, 2, 4, 128)
    kv4p = cat("kv4p").reshape(1, 8, 512, 2, 4, 128)
    kv16p = cat("kv16p").reshape(1, 8, 2048, 2, 4, 128)
    stp_ = cat("stp").reshape(1, 8, 4, 128, 256)
    ccat = lambda name: np.concatenate([np.asarray(r[name]) for r in R], axis=0)
    kv1s = ccat("kv1s").reshape(1, 128, 1, 2, 4, 128)
    kv4s = ccat("kv4s").reshape(1, 128, 1, 2, 4, 128)
    kv16s = ccat("kv16s").reshape(1, 128, 1, 2, 4, 128)
    sts_ = ccat("sts").reshape(1, 128, 4, 128, 256)
    return (y_p.astype(f32), y_s.astype(f32), kv1p, kv4p, kv16p, stp_, kv1s, kv4s, kv16s, sts_)
```
